# Optimizing a Trainium2 kernel written in Bass

```python
import jax, jax.numpy as jnp
from jax import lax
import numpy as np

D_MODEL = 1024
BATCH = 8
SEQ = 2048
DEPTH = 4

CHUNK = 64
SPATIAL_BLOCK = 128
MIX_WIDTH = 2 * D_MODEL
GMLP_WIDTH = MIX_WIDTH // 2
GMLP_GROUPS = 8
GMLP_GROUP_DIM = GMLP_WIDTH // GMLP_GROUPS
GDN_HEAD_DIM = 128
GDN_WIDTH = MIX_WIDTH - GMLP_WIDTH
GDN_HEADS = GDN_WIDTH // GDN_HEAD_DIM
CONV_K = 4
PLE_DIM = 256
EPS = 1e-6

OFF_UA = 0
OFF_VA = OFF_UA + GMLP_WIDTH
OFF_ZA = OFF_VA + GMLP_WIDTH
OFF_QKV = OFF_ZA + GMLP_WIDTH
OFF_ZB = OFF_QKV + 3 * GDN_WIDTH
OFF_A = OFF_ZB + GDN_WIDTH
OFF_B = OFF_A + GDN_HEADS
IN_DIM = OFF_B + GDN_HEADS

kernel_name = "hybrid_gmlp_gdn_streaming_trunk"


def rmsnorm(x, w):
    xf = x.astype(jnp.float32)
    y = xf * lax.rsqrt(jnp.mean(xf * xf, axis=-1, keepdims=True) + EPS)
    return (y * w.astype(jnp.float32)).astype(x.dtype)


def layernorm(x, g, b):
    xf = x.astype(jnp.float32)
    mu = jnp.mean(xf, axis=-1, keepdims=True)
    var = jnp.mean(jnp.square(xf - mu), axis=-1, keepdims=True)
    y = (xf - mu) * lax.rsqrt(var + EPS)
    return (y * g.astype(jnp.float32) + b.astype(jnp.float32)).astype(x.dtype)


def l2norm(t):
    return t * lax.rsqrt(jnp.sum(t * t, axis=-1, keepdims=True) + EPS)


def causal_conv(x, w):
    S = x.shape[1]
    xp = jnp.pad(x, ((0, 0), (CONV_K - 1, 0), (0, 0)))
    return sum(xp[:, j:j + S, :] * w[j] for j in range(CONV_K))


def gmlp_branch(u, v, z, ln_g, ln_b, w_s, b_s):
    Bsz, S, _ = v.shape
    u = jax.nn.gelu(u)
    v = layernorm(jax.nn.gelu(v), ln_g, ln_b)
    pos_chunk = jnp.arange(SPATIAL_BLOCK) // CHUNK
    mask = pos_chunk[None, :] <= pos_chunk[:, None]
    w = jnp.where(mask, w_s, jnp.zeros_like(w_s))
    vb = v.reshape(Bsz, S // SPATIAL_BLOCK, SPATIAL_BLOCK, GMLP_GROUPS, GMLP_GROUP_DIM)
    mixed = jnp.einsum('gij,bnjgc->bnigc', w, vb) + b_s.T[None, None, :, :, None]
    mixed = mixed.reshape(Bsz, S, GMLP_WIDTH)
    return u * mixed * jax.nn.silu(z)


def chunk_gated_delta_rule(q, k, v, g, beta):
    C = q.shape[-2]
    causal = jnp.tril(jnp.ones((C, C), dtype=bool))
    strict = jnp.tril(jnp.ones((C, C), dtype=bool), -1)
    decay = jnp.cumsum(g, axis=-1)
    L = jnp.exp(jnp.where(causal, decay[..., :, None] - decay[..., None, :], -jnp.inf))
    k_beta = k * beta[..., None]
    A = jnp.where(strict, jnp.einsum('bhnik,bhnjk->bhnij', k_beta, k) * L, 0.0)
    eye = jnp.eye(C, dtype=q.dtype)
    T = lax.linalg.triangular_solve(A + eye, jnp.broadcast_to(eye, A.shape),
                                    left_side=True, lower=True, unit_diagonal=True)
    u = jnp.einsum('bhnij,bhnjv->bhniv', T, v * beta[..., None])
    w = jnp.einsum('bhnij,bhnjk->bhnik', T, k_beta * jnp.exp(decay)[..., None])
    qk = jnp.einsum('bhnik,bhnjk->bhnij', q, k) * L
    q_dec = q * jnp.exp(decay)[..., None]
    k_dec = k * jnp.exp(decay[..., -1:] - decay)[..., None]
    chunk_decay = jnp.exp(decay[..., -1])
    xs = tuple(jnp.moveaxis(t, 2, 0) for t in (q_dec, k_dec, u, w, qk, chunk_decay))

    def step(state, inp):
        q_c, k_c, u_c, w_c, qk_c, a_c = inp
        v_new = u_c - jnp.einsum('bhck,bhkv->bhcv', w_c, state)
        o_c = (jnp.einsum('bhck,bhkv->bhcv', q_c, state)
               + jnp.einsum('bhij,bhjv->bhiv', qk_c, v_new))
        state = state * a_c[..., None, None] + jnp.einsum('bhck,bhcv->bhkv', k_c, v_new)
        return state, o_c

    state0 = jnp.zeros(q.shape[:2] + (q.shape[-1], v.shape[-1]), q.dtype)
    _, o = lax.scan(step, state0, xs)
    return jnp.moveaxis(o, 0, 2)


def gdn_branch(qkv, z, a, b, conv_w, A_log, dt_bias, norm_w):
    Bsz, S, _ = qkv.shape
    N = S // CHUNK
    out_dtype = qkv.dtype
    qkv = jax.nn.silu(causal_conv(qkv, conv_w)).astype(jnp.float32)
    q, k, v = jnp.split(qkv, 3, axis=-1)

    def heads(t):
        return t.reshape(Bsz, N, CHUNK, GDN_HEADS, GDN_HEAD_DIM).transpose(0, 3, 1, 2, 4)

    def per_head(t):
        return t.reshape(Bsz, N, CHUNK, GDN_HEADS).transpose(0, 3, 1, 2)

    q = l2norm(heads(q)) * (GDN_HEAD_DIM ** -0.5)
    k = l2norm(heads(k))
    v = heads(v)
    beta = per_head(jax.nn.sigmoid(b.astype(jnp.float32)))
    g = per_head(-jnp.exp(A_log.astype(jnp.float32))
                 * jax.nn.softplus(a.astype(jnp.float32) + dt_bias.astype(jnp.float32)))
    o = chunk_gated_delta_rule(q, k, v, g, beta)
    o = o.transpose(0, 2, 3, 1, 4).reshape(Bsz, S, GDN_HEADS, GDN_HEAD_DIM)
    o = o * lax.rsqrt(jnp.mean(o * o, axis=-1, keepdims=True) + EPS) * norm_w.astype(jnp.float32)
    o = o * jax.nn.silu(z.astype(jnp.float32).reshape(Bsz, S, GDN_HEADS, GDN_HEAD_DIM))
    return o.reshape(Bsz, S, GDN_WIDTH).astype(out_dtype)


def setup_inputs(seed: int = 0) -> dict:
    key = jax.random.key(seed)
    ks = jax.random.split(key, 20)
    f32 = jnp.float32
    nrm = lambda k, shape, scale: jax.random.normal(k, shape, f32) * scale
    dt = jnp.exp(jax.random.uniform(ks[9], (DEPTH, GDN_HEADS), f32, np.log(1e-3), np.log(1e-1)))
    return {
        "x": nrm(ks[0], (BATCH, SEQ, D_MODEL), 1.0),
        "p": nrm(ks[1], (DEPTH, BATCH, SEQ, PLE_DIM), 1.0),
        "norm_w": 1.0 + nrm(ks[2], (DEPTH, D_MODEL), 0.02),
        "w_in": nrm(ks[3], (DEPTH, D_MODEL, IN_DIM), D_MODEL ** -0.5),
        "ln_v_g": 1.0 + nrm(ks[4], (DEPTH, GMLP_WIDTH), 0.02),
        "ln_v_b": nrm(ks[5], (DEPTH, GMLP_WIDTH), 0.01),
        "w_spatial": nrm(ks[6], (DEPTH, GMLP_GROUPS, SPATIAL_BLOCK, SPATIAL_BLOCK), SPATIAL_BLOCK ** -0.5),
        "b_spatial": 1.0 + nrm(ks[7], (DEPTH, GMLP_GROUPS, SPATIAL_BLOCK), 0.01),
        "conv_w": nrm(ks[8], (DEPTH, CONV_K, 3 * GDN_WIDTH), CONV_K ** -0.5),
        "A_log": jnp.log(jax.random.uniform(ks[10], (DEPTH, GDN_HEADS), f32, 1.0, 16.0)),
        "dt_bias": dt + jnp.log(-jnp.expm1(-dt)),
        "gdn_norm_w": 1.0 + nrm(ks[11], (DEPTH, GDN_HEAD_DIM), 0.02),
        "w_out": nrm(ks[12], (DEPTH, MIX_WIDTH, D_MODEL), MIX_WIDTH ** -0.5),
        "w_ple": nrm(ks[13], (DEPTH, PLE_DIM, D_MODEL), PLE_DIM ** -0.5),
        "ple_norm_w": 1.0 + nrm(ks[14], (DEPTH, D_MODEL), 0.02),
        "ple_gate_norm_w": 1.0 + nrm(ks[15], (DEPTH, D_MODEL), 0.02),
        "w_ple_gate": nrm(ks[16], (DEPTH, D_MODEL, D_MODEL), D_MODEL ** -0.5),
        "final_norm_w": 1.0 + nrm(ks[17], (D_MODEL,), 0.02),
    }


def reference(x, p, norm_w, w_in, ln_v_g, ln_v_b, w_spatial, b_spatial, conv_w, A_log, dt_bias,
              gdn_norm_w, w_out, w_ple, ple_norm_w, ple_gate_norm_w, w_ple_gate, final_norm_w):
    h = x
    for i in range(DEPTH):
        xn = rmsnorm(h, norm_w[i])
        proj = jnp.einsum('bsd,de->bse', xn, w_in[i])
        y_a = gmlp_branch(proj[..., OFF_UA:OFF_VA], proj[..., OFF_VA:OFF_ZA], proj[..., OFF_ZA:OFF_QKV],
                          ln_v_g[i], ln_v_b[i], w_spatial[i], b_spatial[i])
        y_b = gdn_branch(proj[..., OFF_QKV:OFF_ZB], proj[..., OFF_ZB:OFF_A],
                         proj[..., OFF_A:OFF_B], proj[..., OFF_B:IN_DIM],
                         conv_w[i], A_log[i], dt_bias[i], gdn_norm_w[i])
        y = jnp.concatenate([y_a, y_b], axis=-1)
        h = h + jnp.einsum('bse,ed->bsd', y, w_out[i])
        e = rmsnorm(jnp.einsum('bsp,pd->bsd', p[i], w_ple[i]), ple_norm_w[i])
        gate = jax.nn.sigmoid(jnp.einsum('bsd,de->bse', rmsnorm(h, ple_gate_norm_w[i]), w_ple_gate[i]))
        h = h + gate * e
    return rmsnorm(h, final_norm_w)
```

```python
import numpy as np
from contextlib import ExitStack
import concourse.bass as bass
import concourse.mybir as mybir
from concourse.bass_utils import run_bass_kernel_spmd

F32 = mybir.dt.float32
BF16 = mybir.dt.bfloat16
AF = mybir.ActivationFunctionType
ALU = mybir.AluOpType

COMPUTE = ("pe", "act", "dve", "pool")

D_MODEL = 1024
SEQ = 2048
NT = 16
ST = 4
NH = 4
NSUP = NT // ST
TS = ST * 128
IN_DIM = 7184
EPS = 1e-6
NEG = -65536.0
GOFF = 11
NBIG = 3
CH_DT = mybir.dt.float32r

K_ID = 0
K_U = 128
K_ONE = 256
K_ES = 384
K_MH = 384 + 1024
K_EPS = K_MH + 1
NKS = K_EPS + 1
K_NC = NKS
K_NS = NKS + 128
NK = NKS + 256
C_NW, C_GNW, C_LNG, C_LNB, C_CONV = 0, 8, 16, 24, 32
NFM = 128
B_PNW, B_GDN, B_DTB, B_ALOG = 0, 1024, 1152, 1160
NBC = 1168


class _Op:
    __slots__ = ("eng", "emit", "reads", "writes", "dma_sem", "idx", "pos", "deps",
                 "need_inc", "tick", "dma_cnt", "know")

    def __init__(self, eng, emit, reads, writes, dma_sem):
        self.eng, self.emit, self.reads, self.writes, self.dma_sem = eng, emit, reads, writes, dma_sem
        self.deps = []
        self.need_inc = False
        self.tick = None
        self.dma_cnt = None
        self.know = None


class Prog:
    def __init__(self, nc, epoch=2000):
        self.nc = nc
        self.ops = []
        self.epoch = epoch

    def op(self, eng, fname, *args, reads=(), writes=(), **kw):
        self.ops.append(_Op(eng, (fname, args, kw), tuple(reads), tuple(writes), None))

    def dma(self, queue, sem, reads=(), writes=(), **kw):
        o = _Op(queue, ("dma_start", (), kw), tuple(reads), tuple(writes), sem)
        self.ops.append(o)
        return o

    def analyze(self):
        last_w, readers, pos_ctr, dma_ctr, know = {}, {}, {}, {}, {}
        for i, o in enumerate(self.ops):
            o.idx = i
            if o.dma_sem is not None:
                src = ("dma", o.dma_sem)
                dma_ctr[src] = dma_ctr.get(src, 0) + 1
                o.pos = dma_ctr[src]
            else:
                pos_ctr[o.eng] = pos_ctr.get(o.eng, 0) + 1
                o.pos = pos_ctr[o.eng]

        def src_of(o):
            return ("dma", o.dma_sem) if o.dma_sem is not None else o.eng

        ops = self.ops
        for o in ops:
            cand = set()
            raw = set()
            for k in o.reads:
                w = last_w.get(k)
                if w is not None:
                    cand.add(w)
                    raw.add(w)
            for k in o.writes:
                w = last_w.get(k)
                if w is not None:
                    cand.add(w)
                for r in readers.get(k, ()):
                    cand.add(r)
            ek = know.setdefault(o.eng, {})
            best = {}
            for p in cand:
                po = ops[p]
                if po is o:
                    continue
                if po.dma_sem is None and o.dma_sem is None and po.eng == o.eng:
                    if o.eng == "pe":
                        continue
                s = src_of(po)
                if po.pos <= ek.get(s, 0):
                    continue
                if s not in best or ops[best[s]].pos < po.pos:
                    best[s] = p
            final = []
            for s, p in sorted(best.items(), key=lambda kv: -kv[1]):
                po = ops[p]
                if po.pos <= ek.get(s, 0):
                    continue
                final.append(p)
                po.need_inc = True
                ek[s] = po.pos
                for s2, v2 in po.know.items():
                    if ek.get(s2, 0) < v2:
                        ek[s2] = v2
            o.deps = final
            o.know = dict(ek)
            for k in o.reads:
                readers.setdefault(k, []).append(o.idx)
            for k in o.writes:
                last_w[k] = o.idx
                readers[k] = []
        tick = {}
        for o in ops:
            if o.dma_sem is not None:
                o.dma_cnt = 16 * o.pos
            elif o.need_inc:
                tick[o.eng] = tick.get(o.eng, 0) + 1
                o.tick = tick[o.eng]
        self.nticks = tick
        self.dma_sems = sorted({o.dma_sem for o in ops if o.dma_sem is not None}, key=str)

    def emit(self, final_waits=()):
        nc = self.nc
        self.analyze()
        ops = self.ops
        with ExitStack() as st:
            esems = {}
            for e in COMPUTE:
                n = self.nticks.get(e, 0)
                ne = (n + self.epoch - 1) // self.epoch
                esems[e] = [st.enter_context(nc.semaphore(f"s_{e}_{j}")) for j in range(ne)]
            dsems = {k: st.enter_context(nc.semaphore(f"d_{i}")) for i, k in enumerate(self.dma_sems)}
            block = st.enter_context(nc.Block())

            def sem_for(po):
                if po.dma_sem is not None:
                    return dsems[po.dma_sem], po.dma_cnt
                t = po.tick - 1
                return esems[po.eng][t // self.epoch], (t % self.epoch) + 1

            def run(engname, engobj):
                for o in ops:
                    if o.eng != engname:
                        continue
                    for p in o.deps:
                        s, v = sem_for(ops[p])
                        engobj.wait_ge(s, v)
                    fn, a, kw = o.emit
                    ins = getattr(engobj, fn)(*a, **kw)
                    if o.dma_sem is not None:
                        ins.then_inc(dsems[o.dma_sem], 16)
                    elif o.need_inc:
                        s, _ = sem_for(o)
                        ins.then_inc(s, 1)
                if engname == "sp":
                    for o in final_waits:
                        s, v = sem_for(o)
                        engobj.wait_ge(s, v)

            @block.tensor
            def _(e):
                run("pe", e)

            @block.scalar
            def _(e):
                run("act", e)

            @block.vector
            def _(e):
                run("dve", e)

            @block.gpsimd
            def _(e):
                run("pool", e)

            @block.sync
            def _(e):
                run("sp", e)


ALL_STAGES = ("consts", "A", "prep", "gdn", "gmlp", "outproj", "tail")


def build(L, emit_h=False, chain_dt=F32, stages=ALL_STAGES, nsup=NSUP, gdn_heads=8, gdn_cut=99):
    nc = bass.Bass("TRN2", target_bir_lowering=False)

    def din(name, shape):
        return nc.dram_tensor(name, list(shape), F32, kind="ExternalInput").ap()

    x_d = din("x", [SEQ, D_MODEL])
    p_d = din("p", [L, SEQ, 256])
    win_d = din("w_in", [L, D_MODEL, IN_DIM])
    wout_d = din("w_out", [L, 2048, D_MODEL])
    wple_d = din("w_ple", [L, 256, D_MODEL])
    wgate_d = din("w_gate", [L, D_MODEL, D_MODEL])
    wst_d = din("wsT", [L, 128, 8, 128])
    bsp_d = din("bsp", [L, 128, 1024])
    cfm_d = din("cfm", [L, 128, NFM])
    cbc_d = din("cbc", [L, 128, NBC])
    fnw_d = din("fnw", [128, D_MODEL])
    konst_d = din("konst", [128, NK])
    out_d = nc.dram_tensor("out", [SEQ, D_MODEL], F32, kind="ExternalOutput").ap()
    hout_d = nc.dram_tensor("hout", [SEQ, D_MODEL], F32, kind="ExternalOutput").ap() if emit_h else None

    with ExitStack() as st:
        def sb(name, shape, dt=F32):
            return st.enter_context(nc.sbuf_tensor("sb_" + name, list(shape), dt))

        H = sb("H", [128, ST, D_MODEL])
        konst = sb("konst", [128, NKS])
        identB = sb("identB", [128, 128], BF16)
        negcB = sb("negcB", [128, 128], BF16)
        negsB = sb("negsB", [128, 128], BF16)
        onesB = sb("onesB", [128, 128], BF16)
        cfm = sb("cfm", [128, NFM])
        cbc = sb("cbc", [128, NBC])
        wab = sb("wab", [128, 8, 16], BF16)
        wsTb = sb("wsTb", [128, 8, 128], BF16)
        Rt = sb("Rt", [128, 8, 128], BF16)
        negA = sb("negA", [128, 8])
        xn = [sb(f"xn{i}", [128, D_MODEL], BF16) for i in range(1)]
        xnT = sb("xnT", [128, 8, TS], BF16)
        ringG = [sb(f"ringG{i}", [128, 4096], BF16) for i in range(2)]
        ringM = [sb(f"ringM{i}", [128, 4096], BF16) for i in range(2)]
        gu = sb("gu", [128, 8, TS], BF16)
        TM1 = sb("TM1", [128, D_MODEL])
        nln = sb("nln", [128, D_MODEL], BF16)
        tmix = sb("tmix", [128, 4, 128])
        gsig = tmix[:].rearrange("p g i -> p (g i)")
        yT = sb("yT", [128, 16, TS], BF16)
        pre = [sb(f"pre{i}", [128, TS + 3], BF16) for i in range(1)]
        dg = [sb(f"dg{i}", [128, 4, 128], BF16) for i in range(2)]
        acc = [sb(f"acc{i}", [128, 256]) for i in range(1)]
        qTb = sb("qTb", [128, 8, TS], BF16)
        kTb = sb("kTb", [128, 8, TS], BF16)
        vTb = sb("vTb", [128, 8, TS], BF16)
        halo = sb("halo", [128, L * 24, 3])
        Gz = sb("Gz", [128, 8, ST, 128], BF16)
        gx1 = sb("gx1", [128, ST, 8])
        gbs = sb("gbs", [128, ST, 8])
        ge1 = sb("ge1", [128, ST, 8])
        gsp = sb("gsp", [128, ST, 8])
        gg = sb("gg", [128, ST, 8])
        glb = sb("glb", [128, ST, 8])
        glnb = sb("glnb", [128, ST, 8])
        gbeta = sb("gbeta", [128, ST, 8])
        gd = sb("gd", [128, ST, 8])
        gnegd = sb("gnegd", [128, ST, 8])
        gEd = sb("gEd", [128, ST, 8])
        gtdl = sb("gtdl", [128, ST, 8])
        gEl = sb("gEl", [128, ST, 8])
        GP = sb("GP", [128, ST, 8, 3])
        class _NS:
            pass
        GT = []
        for gi in range(8 // NH):
            t_ = _NS()
            t_.dT3 = sb(f"dT3_{gi}", [8, 256])
            t_.sqs = sb(f"sqs_{gi}", [128, 2 * NH, 128], BF16)
            for nm in ("kn_tm", "kbd", "kdec", "vb", "knT", "Pb", "QKT", "LcT"):
                setattr(t_, nm, sb(f"{nm}_{gi}", [128, NH, 128], BF16))
            t_.LbT = sb(f"LbT_{gi}", [128, NH, 128])
            t_.Bc = sb(f"Bc_{gi}", [128, NH, 128], CH_DT)
            t_.Ac = sb(f"Ac_{gi}", [128, NH, 128], CH_DT)
            t_.Pc = sb(f"Pc_{gi}", [128, NH, 128], CH_DT)
            t_.osb = t_.LbT
            t_.st3 = sb(f"st3_{gi}", [128, 48])
            GT.append(t_)
        sqs = GT[0].sqs
        stn = sb("stn", [128, 16])
        SstL = [sb(f"Sst{i}", [128, 8, 128]) for i in range(L)]
        Sb = sb("Sb", [128, 8, 128], BF16)
        pT = xn[0][:].rearrange("p (k j) -> p k j", k=2)

        ps = [st.enter_context(nc.psum_tensor(f"ps{i}", [128, 512], F32)) for i in range(8)]

        P = Prog(nc)
        big_ctr = [0]
        small_ctr = [0]

        def nbig():
            b = big_ctr[0] % NBIG
            big_ctr[0] += 1
            return b

        def nsm():
            b = NBIG + small_ctr[0] % (8 - NBIG)
            small_ctr[0] += 1
            return b

        def PK(b):
            return ("ps", b)

        identF = konst[:, K_ID:K_ID + 128]
        Umat = konst[:, K_U:K_U + 128]
        onesF = konst[:, K_ONE:K_ONE + 128]
        mhalf = konst[:, K_MH:K_MH + 1]
        epsc = konst[:, K_EPS:K_EPS + 1]

        def mmv(ap):
            return ap

        def esel(h):
            return konst[0:8, K_ES + h * 128:K_ES + (h + 1) * 128]

        def rsqrt_cols(dst, src, n, rkeys, wkeys):
            P.op("pool", "tensor_tensor", out=dst, in0=src, in1=mhalf.to_broadcast([128, n]), op=ALU.pow,
                 reads=list(rkeys) + ["konst"], writes=wkeys)

        P.dma("sp", "c_konst", out=konst[:], in_=konst_d[:, 0:NKS], writes=["konst"])
        P.op("dve", "tensor_copy", out=identB[:], in_=identF, reads=["konst"], writes=["identB"])
        P.dma("pool", "c_negc", out=negcB[:], in_=konst_d[:, K_NC:K_NC + 128], writes=["negcB"])
        P.dma("pool", "c_negs", out=negsB[:], in_=konst_d[:, K_NS:K_NS + 128], writes=["negsB"])
        P.op("dve", "tensor_copy", out=onesB[:], in_=onesF, reads=["konst"], writes=["onesB"])
        P.op("dve", "memset", GP[:], 1.0, writes=["GP"])
        xv = x_d.rearrange("(t p) d -> p t d", p=128)
        for li in range(L):
            P.op("dve", "memset", SstL[li][:], 0.0, writes=[("S", li, h) for h in range(8)])
        P.op("dve", "memset", halo[:], 0.0, writes=[("halo", b_) for b_ in range(L * 24)])

        gslot = [0]
        mslot = [0]
        cur_l = [0]
        outs = []

        def load_G(l, hd):
            sl = gslot[0] % 2
            gslot[0] += 1
            dst = ringG[sl][:].rearrange("p (k c j) -> p k c j", k=8, c=4)
            src = win_d[l].rearrange("(k p) e -> p k e", p=128)[:, :, 3072:7168].rearrange(
                "p k (c h j) -> p k c h j", c=4, h=8)[:, :, :, hd, :]
            for c in range(4):
                P.dma("pool", ("rG", sl, c, l % 2), out=dst[:, :, c, :], in_=src[:, :, c, :], writes=[("ringG", sl, c)])
            return sl

        def load_Gx(src_ap, shape3):
            sl = gslot[0] % 2
            gslot[0] += 1
            k, c = shape3
            dst = ringG[sl][:, 0:k * c].rearrange("p (k c) -> p k c", k=k)
            keys = [("ringG", sl, q) for q in range(4)]
            P.dma("pool", ("rGx", sl, cur_l[0] % 2), out=dst, in_=src_ap, writes=keys)
            return keys, dst

        def load_M(src_ap, shape3):
            sl = mslot[0] % 2
            mslot[0] += 1
            k, c = shape3
            dst = ringM[sl][:, 0:k * c].rearrange("p (k c) -> p k c", k=k)
            P.dma("pool", ("rM", sl, cur_l[0] % 2), out=dst, in_=src_ap, writes=[("ringM", sl)])
            return sl, dst

        def norm_to_T(l, T, tt, col0, dstT):
            xb = xn[0]
            xk = ("xn", 0)
            P.op("act", "activation", out=xb[:], in_=H[:, T, :], func=AF.Square, accum_out=stn[:, 0:1],
                 reads=[("H", T)], writes=[xk, "stn0"])
            P.op("dve", "tensor_scalar", out=stn[:, 1:2], in0=stn[:, 0:1], scalar1=1.0 / D_MODEL, scalar2=EPS,
                                                  op0=ALU.mult, op1=ALU.add, reads=["stn0"], writes=["stn1"])
            rsqrt_cols(stn[:, 2:3], stn[:, 1:2], 1, ["stn1"], ["stn2"])
            P.op("act", "activation", out=xb[:], in_=H[:, T, :], func=AF.Copy, scale=stn[:, 2:3],
                 reads=[("H", T), "stn2"], writes=[xk])
            b = nsm()
            tpv = ps[b][:, :].bitcast(BF16).rearrange("p (k j) -> p k j", k=8)
            for kc in range(8):
                P.op("pe", "transpose", tpv[:, kc, :], xb[:, kc * 128:(kc + 1) * 128], identB[:],
                     reads=[xk, "identB"], writes=[PK(b)])
            P.op("dve", "tensor_tensor", out=dstT[:, :, tt * 128:(tt + 1) * 128], in0=tpv,
                                                  in1=cfm[:, col0:col0 + 8].unsqueeze(2).to_broadcast([128, 8, 128]),
                                                  op=ALU.mult,
                 reads=["cfm"], writes=[PK(b), ("xnT", tt)])

        for s in range(nsup):
            P.dma("sp", ("ldx", s % 2), out=H[:, :, :], in_=xv[:, s * ST:(s + 1) * ST, :], writes=[("H", t) for t in range(ST)])
            for l in range(L):
                cur_l[0] = l
                Sst = SstL[l]
                if "consts" in stages:
                    P.dma("sp", "c_cfm", out=cfm[:], in_=cfm_d[l], writes=["cfm"])
                    P.dma("sp", "c_cbc", out=cbc[:], in_=cbc_d[l], writes=["cbc"])
                    P.dma("pool", "c_rt", out=Rt[:].rearrange("p g i -> p (g i)"), in_=bsp_d[l], writes=["Rt"])
                    P.dma("pool", "c_wab", out=wab[:], in_=win_d[l].rearrange("(k p) e -> p k e", p=128)[:, :, 7168:7184], writes=["wab"])
                    P.dma("pool", "c_wst", out=wsTb[:], in_=wst_d[l], writes=["wsTb"])
                    P.op("pool", "memset", wsTb[64:128, :, 0:64], 0.0, reads=["wsTb"], writes=["wsTb"])
                    for half in range(2):
                        b = nbig()
                        for gq in range(4):
                            g = half * 4 + gq
                            P.op("pe", "matmul", ps[b][:, gq * 128:(gq + 1) * 128], lhsT=onesB[:],
                                                                             rhs=wsTb[:, g, :], start=True, stop=True,
                                 reads=["onesB", "wsTb"], writes=[PK(b)])
                        for gq in range(4):
                            g = half * 4 + gq
                            P.op("dve", "scalar_tensor_tensor", out=Rt[:, g, :], in0=ps[b][:, gq * 128:(gq + 1) * 128], scalar=cfm[:, C_LNB + g:C_LNB + g + 1],
                                in1=Rt[:, g, :], op0=ALU.mult, op1=ALU.add, reads=["cfm", "Rt"], writes=[PK(b), "Rt"])
                    P.op("act", "activation", out=negA[:], in_=cbc[:, B_ALOG:B_ALOG + 8], func=AF.Exp, reads=["cbc"], writes=["negA"])
                    P.op("dve", "tensor_scalar", out=negA[:], in0=negA[:], scalar1=-1.0, scalar2=None, op0=ALU.mult,
                         reads=["negA"], writes=["negA"])
                    P.op("act", "copy", out=Sb[:], in_=Sst[:], reads=[("S", l, h) for h in range(8)], writes=[("Sb", h) for h in range(8)])

                T0 = 0
                if "A" in stages:
                    for tt in range(ST):
                        norm_to_T(l, T0 + tt, tt, C_NW, xnT)
                XT = [("xnT", tt) for tt in range(ST)]

                if "prep" in stages:
                    b = nsm()
                    for tt in range(ST):
                        for kc in range(8):
                            P.op("pe", "matmul", ps[b][:, tt * 16:(tt + 1) * 16], lhsT=xnT[:, kc, tt * 128:(tt + 1) * 128], rhs=wab[:, kc, :],
                                start=(kc == 0), stop=(kc == 7), reads=[("xnT", tt), "wab"], writes=[PK(b)])
                    abv = ps[b][:, 0:ST * 16].rearrange("p (t c) -> p t c", c=16)
                    P.op("dve", "tensor_tensor", out=gx1[:], in0=abv[:, :, 0:8],
                                                          in1=cbc[:, B_DTB:B_DTB + 8].unsqueeze(1).to_broadcast([128, ST, 8]),
                                                          op=ALU.add, reads=["cbc"], writes=[PK(b), "gx1"])
                    P.op("dve", "tensor_copy", out=gbs[:], in_=abv[:, :, 8:16], writes=[PK(b), "gbs"])
                    P.op("act", "activation", out=ge1[:], in_=gx1[:], func=AF.Exp, reads=["gx1"], writes=["ge1"])
                    P.op("act", "activation", out=gsp[:], in_=ge1[:], func=AF.Ln, bias=1.0, reads=["ge1"], writes=["gsp"])
                    P.op("dve", "tensor_tensor", out=gg[:], in0=gsp[:], in1=negA[:].unsqueeze(1).to_broadcast([128, ST, 8]),
                                                          op=ALU.mult, reads=["gsp", "negA"], writes=["gg"])
                    P.op("act", "activation", out=ge1[:], in_=gbs[:], func=AF.Exp, scale=-1.0, reads=["gbs"], writes=["ge1"])
                    P.op("act", "activation", out=glb[:], in_=ge1[:], func=AF.Ln, bias=1.0, reads=["ge1"], writes=["glb"])
                    P.op("act", "activation", out=gbeta[:], in_=glb[:], func=AF.Exp, scale=-1.0, reads=["glb"], writes=["gbeta"])
                    P.op("dve", "tensor_scalar", out=glnb[:], in0=glb[:], scalar1=-1.0, scalar2=None, op0=ALU.mult,
                         reads=["glb"], writes=["glnb"])
                    bd = nsm()
                    bl = nsm()
                    for tt in range(ST):
                        P.op("pe", "matmul", ps[bd][:, tt * 8:(tt + 1) * 8], lhsT=Umat, rhs=gg[:, tt, :],
                                                                     start=True, stop=True, reads=["konst", "gg"], writes=[PK(bd)])
                    for tt in range(ST):
                        P.op("pe", "matmul", ps[bl][:, tt * 8:(tt + 1) * 8], lhsT=onesF, rhs=gg[:, tt, :],
                                                                     start=True, stop=True, reads=["konst", "gg"], writes=[PK(bl)])
                    gdf = gd[:].rearrange("p t h -> p (t h)")
                    P.op("dve", "tensor_copy", out=gdf, in_=ps[bd][:, 0:ST * 8], writes=[PK(bd), "gd"])
                    P.op("dve", "tensor_scalar", out=gnegd[:], in0=gd[:], scalar1=-1.0, scalar2=None, op0=ALU.mult,
                         reads=["gd"], writes=["gnegd"])
                    P.op("act", "activation", out=gEd[:], in_=gd[:], func=AF.Exp, reads=["gd"], writes=["gEd"])
                    P.op("dve", "tensor_tensor", out=gtdl[:].rearrange("p t h -> p (t h)"), in0=ps[bl][:, 0:ST * 8], in1=gdf,
                                                          op=ALU.subtract, reads=["gd"], writes=[PK(bl), "gtdl"])
                    P.op("act", "activation", out=gEl[:].rearrange("p t h -> p (t h)"), in_=ps[bl][:, 0:ST * 8], func=AF.Exp,
                         writes=[PK(bl), "gEl"])
                    P.op("act", "activation", out=GP[:, :, :, 2], in_=gtdl[:], func=AF.Exp, reads=["gtdl"], writes=["GP"])
                    P.op("dve", "tensor_tensor", out=GP[:, :, :, 1], in0=gbeta[:], in1=gEd[:], op=ALU.mult,
                         reads=["gbeta", "gEd", "GP"], writes=["GP"])

                def gdn_phase1():
                    nxt = load_G(l, 0)
                    for hd in range(8):
                        hq = hd
                        sl = nxt
                        if hd + 1 < 8:
                            nxt = load_G(l, hd + 1)
                        Wc = ringG[sl][:].rearrange("p (k c j) -> p k c j", k=8, c=4)
                        bz = nsm()
                        for tt in range(ST):
                            for kc in range(8):
                                P.op("pe", "matmul", ps[bz][:, tt * 128:(tt + 1) * 128], lhsT=xnT[:, kc, tt * 128:(tt + 1) * 128],
                                     rhs=Wc[:, kc, 3, :], start=(kc == 0), stop=(kc == 7),
                                     reads=[("xnT", tt), ("ringG", sl, 3)], writes=[PK(bz)])
                        P.op("act", "activation", out=Gz[:, hq, :, :].rearrange("p t j -> p (t j)"), in_=ps[bz][:, :], func=AF.Silu,
                             writes=[PK(bz), ("Gz", hq)])
                        P.op("dve", "tensor_tensor", out=Gz[:, hq, :, :], in0=Gz[:, hq, :, :],
                             in1=cbc[:, B_GDN:B_GDN + 128].unsqueeze(1).to_broadcast([128, ST, 128]), op=ALU.mult,
                             reads=[("Gz", hq), "cbc"], writes=[("Gz", hq)])
                        yield
                        for c3, dst, dk in ((0, qTb, "qTb"), (1, kTb, "kTb"), (2, vTb, "vTb")):
                            blk = c3 * 8 + hd
                            hb = l * 24 + blk
                            bb = nbig()
                            for kc in range(8):
                                P.op("pe", "matmul", ps[bb][:, :], lhsT=Wc[:, kc, c3, :], rhs=xnT[:, kc, :], start=(kc == 0), stop=(kc == 7),
                                     reads=XT + [("ringG", sl, c3)], writes=[PK(bb)])
                            pr = pre[0]
                            di = blk % 2
                            P.op("act", "copy", out=pr[:, 3:TS + 3], in_=ps[bb][:, :], writes=[PK(bb), ("pre", 0)])
                            P.op("dve", "tensor_copy", out=pr[:, 0:3], in_=halo[:, hb, :], reads=[("halo", hb)], writes=[("preh", 0)])
                            P.op("dve", "tensor_copy", out=halo[:, hb, :], in_=pr[:, TS:TS + 3], reads=[("pre", 0)], writes=[("halo", hb)])
                            for tap in range(4):
                                c_ = C_CONV + tap * 24 + blk
                                P.op("dve", "tensor_scalar", out=dg[di][:, tap, :], in0=identB[:], scalar1=cfm[:, c_:c_ + 1], scalar2=None, op0=ALU.mult,
                                     reads=["identB", "cfm"], writes=[("dg", di)])
                            bc = nbig()
                            for tap in range(4):
                                P.op("pe", "matmul", ps[bc][:, :], lhsT=dg[di][:, tap, :], rhs=pr[:, tap:tap + TS], start=(tap == 0), stop=(tap == 3),
                                     reads=[("dg", di), ("pre", 0), ("preh", 0)], writes=[PK(bc)])
                            P.op("act", "activation", out=dst[:, hq, :], in_=ps[bc][:, :], func=AF.Silu, writes=[PK(bc), (dk, hq)])
                            yield

                def gdn_tiles(gr):
                    h0 = gr * NH
                    hsl = slice(h0, h0 + NH)
                    G_ = GT[gr]
                    dT3, sqs, kn_tm, kbd, kdec, vb, knT, Pb, QKT, LcT = G_.dT3, G_.sqs, G_.kn_tm, G_.kbd, G_.kdec, G_.vb, G_.knT, G_.Pb, G_.QKT, G_.LcT
                    LbT, Bc, Ac, Pc, osb, st3 = G_.LbT, G_.Bc, G_.Ac, G_.Pc, G_.osb, G_.st3
                    if True:
                        QK_ = [("qTb", h0 + q) for q in range(NH)]
                        KK_ = [("kTb", h0 + q) for q in range(NH)]
                        VK_ = [("vTb", h0 + q) for q in range(NH)]
                        GZ_ = [("Gz", h0 + q) for q in range(NH)]
                        for tt in range(ST):
                            tsl = slice(tt * 128, (tt + 1) * 128)
                            bt = nsm()
                            P.op("pe", "matmul", ps[bt][0:8, 0:128], lhsT=gg[:, tt, :], rhs=Umat, start=True, stop=True,
                                 reads=["gg", "konst"], writes=[PK(bt)])
                            P.op("pe", "matmul", ps[bt][0:8, 128:256], lhsT=gg[:, tt, :], rhs=Umat, start=True, stop=False,
                                 reads=["gg", "konst"], writes=[PK(bt)])
                            P.op("pe", "matmul", ps[bt][0:8, 128:256], lhsT=glnb[:, tt, :], rhs=identF, start=False, stop=True,
                                 reads=["glnb", "konst"], writes=[PK(bt)])
                            P.op("act", "copy", out=dT3[:, 0:256], in_=ps[bt][0:8, 0:256], writes=[PK(bt), ("dT3a", gr)])
                            P.op("act", "activation", out=sqs[:, 0:NH, :], in_=kTb[:, hsl, tsl], func=AF.Square, reads=KK_, writes=[("sqs", gr)])
                            P.op("act", "activation", out=sqs[:, NH:2 * NH, :], in_=qTb[:, hsl, tsl], func=AF.Square, reads=QK_, writes=[("sqs", gr)])
                            yield
                            bs_ = nsm()
                            for j in range(2 * NH):
                                P.op("pe", "matmul", ps[bs_][:, j:j + 1], lhsT=sqs[:, j, :], rhs=onesB[:, 0:1], start=True, stop=True,
                                     reads=[("sqs", gr), "onesB"], writes=[PK(bs_)])
                            P.op("dve", "tensor_scalar", out=st3[:, 0:2 * NH], in0=ps[bs_][:, 0:2 * NH], scalar1=EPS, scalar2=None, op0=ALU.add,
                                 writes=[PK(bs_), ("st3_a", gr)])
                            rsqrt_cols(st3[:, 8:8 + 2 * NH], st3[:, 0:2 * NH], 2 * NH, [("st3_a", gr)], [("st3_r", gr)])
                            sc3 = st3[:, 16:16 + 3 * NH].rearrange("p (h c) -> p h c", c=3)
                            P.op("dve", "tensor_tensor", out=sc3, in0=GP[:, tt, hsl, :], in1=st3[:, 8:8 + NH].unsqueeze(2).to_broadcast([128, NH, 3]),
                                 op=ALU.mult, reads=["GP", ("st3_r", gr)], writes=[("st3_sc", gr)])
                            b1 = nsm()
                            v1 = ps[b1][:, 0:128 * NH].bitcast(BF16).rearrange("p (h c j) -> p h c j", h=NH, c=2)
                            for hq in range(NH):
                                P.op("pe", "transpose", v1[:, hq, 0, :], kTb[:, h0 + hq, tsl], identB[:], reads=[("kTb", h0 + hq), "identB"], writes=[PK(b1)])
                                P.op("pe", "transpose", v1[:, hq, 1, :], vTb[:, h0 + hq, tsl], identB[:], reads=[("vTb", h0 + hq), "identB"], writes=[PK(b1)])
                            for ci, (dstt, dkey) in enumerate(((kn_tm, ("kn_tm", gr)), (kbd, ("kbd", gr)), (kdec, ("kdec", gr)))):
                                P.op("dve", "tensor_tensor", out=dstt[:], in0=v1[:, :, 0, :], in1=sc3[:, :, ci].unsqueeze(2).to_broadcast([128, NH, 128]),
                                     op=ALU.mult, reads=[("st3_sc", gr)], writes=[PK(b1), dkey])
                            for hq in range(NH):
                                P.op("act", "activation", out=vb[:, hq, :], in_=v1[:, hq, 1, :], func=AF.Copy, scale=gbeta[:, tt, h0 + hq:h0 + hq + 1],
                                     reads=["gbeta"], writes=[PK(b1), ("vb", gr)])
                            yield
                            b2 = nsm()
                            v2 = ps[b2][:, 0:64 * NH].bitcast(BF16).rearrange("p (h j) -> p h j", h=NH)
                            for hq in range(NH):
                                P.op("pe", "transpose", v2[:, hq, :], kn_tm[:, hq, :], identB[:], reads=[("kn_tm", gr), "identB"], writes=[PK(b2)])
                            P.op("act", "copy", out=knT[:], in_=v2, writes=[PK(b2), ("knT", gr)])
                            yield
                            b3 = nsm()
                            for hq in range(NH):
                                P.op("pe", "matmul", ps[b3][:, hq * 128:(hq + 1) * 128], lhsT=knT[:, hq, :], rhs=knT[:, hq, :], start=True, stop=True,
                                     reads=[("knT", gr)], writes=[PK(b3)])
                            b4 = nsm()
                            for hq in range(NH):
                                o4 = ps[b4][:, hq * 128:(hq + 1) * 128]
                                P.op("pe", "matmul", o4, lhsT=esel(h0 + hq), rhs=dT3[0:8, 128:256], start=True, stop=False,
                                     reads=["konst", ("dT3a", gr)], writes=[PK(b4)])
                                P.op("pe", "matmul", o4, lhsT=identB[:], rhs=negsB[:], start=False, stop=True,
                                     reads=["identB", "negsB"], writes=[PK(b4)])
                            LbTf = LbT[:].rearrange("p h j -> p (h j)")
                            for hq in range(NH):
                                P.op("act", "activation", out=LbT[:, hq, :], in_=ps[b4][:, hq * 128:(hq + 1) * 128], func=AF.Exp,
                                     bias=gnegd[:, tt, h0 + hq:h0 + hq + 1], reads=["gnegd"], writes=[PK(b4), ("LbT", gr)])
                            Bcf = Bc[:].rearrange("p h j -> p (h j)")
                            Acf = Ac[:].rearrange("p h j -> p (h j)")
                            Pcf = Pc[:].rearrange("p h j -> p (h j)")
                            P.op("dve", "scalar_tensor_tensor", out=Bcf, in0=ps[b3][:, 0:128 * NH], scalar=-1.0, in1=LbTf, op0=ALU.mult, op1=ALU.mult,
                                 reads=[("LbT", gr)], writes=[PK(b3), ("Bc", gr)])
                            yield
                            b5 = nsm()
                            for hq in range(NH):
                                P.op("pe", "transpose", ps[b5][:, hq * 128:(hq + 1) * 128], Bc[:, hq, :].bitcast(F32), identF, reads=[("Bc", gr), "konst"], writes=[PK(b5)])
                            P.op("act", "copy", out=Acf, in_=ps[b5][:, 0:128 * NH], writes=[PK(b5), ("Ac", gr)])
                            P.op("dve", "tensor_tensor", out=Pc[:], in0=Bc[:].bitcast(F32), in1=identF.unsqueeze(1).to_broadcast([128, NH, 128]), op=ALU.add,
                                 reads=[("Bc", gr), "konst"], writes=[("Pc", gr)])
                            yield
                            for lev in range(6):
                                last = (lev == 5)
                                if not last:
                                    bB = nsm()
                                    for hq in range(NH):
                                        P.op("pe", "matmul", ps[bB][:, hq * 128:(hq + 1) * 128], lhsT=mmv(Ac[:, hq, :]), rhs=mmv(Bc[:, hq, :]), start=True, stop=True,
                                             reads=[("Ac", gr), ("Bc", gr)], writes=[PK(bB)])
                                bA = nsm()
                                for hq in range(NH):
                                    P.op("pe", "matmul", ps[bA][:, hq * 128:(hq + 1) * 128], lhsT=mmv(Bc[:, hq, :]), rhs=mmv(Ac[:, hq, :]), start=True, stop=True,
                                         reads=[("Ac", gr), ("Bc", gr)], writes=[PK(bA)])
                                if not last:
                                    P.op("act", "copy", out=Bcf, in_=ps[bB][:, 0:128 * NH], writes=[PK(bB), ("Bc", gr)])
                                P.op("dve", "tensor_copy", out=Acf, in_=ps[bA][:, 0:128 * NH], writes=[PK(bA), ("Ac", gr)])
                                yield
                                bP = nsm()
                                for hq in range(NH):
                                    P.op("pe", "matmul", ps[bP][:, hq * 128:(hq + 1) * 128], lhsT=mmv(Ac[:, hq, :]), rhs=mmv(Pc[:, hq, :]), start=True, stop=True,
                                         reads=[("Ac", gr), ("Pc", gr)], writes=[PK(bP)])
                                if not last:
                                    P.op("dve", "tensor_tensor", out=Pcf, in0=Pcf.bitcast(F32), in1=ps[bP][:, 0:128 * NH], op=ALU.add,
                                         reads=[("Pc", gr)], writes=[PK(bP), ("Pc", gr)])
                                else:
                                    P.op("dve", "tensor_tensor", out=Pb[:].rearrange("p h j -> p (h j)"), in0=Pcf.bitcast(F32), in1=ps[bP][:, 0:128 * NH], op=ALU.add,
                                         reads=[("Pc", gr)], writes=[PK(bP), ("Pb", gr)])
                                yield
                            b6 = nsm()
                            for hq in range(NH):
                                o6 = ps[b6][:, hq * 128:(hq + 1) * 128]
                                P.op("pe", "matmul", o6, lhsT=esel(h0 + hq), rhs=dT3[0:8, 0:128], start=True, stop=False,
                                     reads=["konst", ("dT3a", gr)], writes=[PK(b6)])
                                P.op("pe", "matmul", o6, lhsT=identB[:], rhs=negcB[:], start=False, stop=True,
                                     reads=["identB", "negcB"], writes=[PK(b6)])
                            for hq in range(NH):
                                P.op("act", "activation", out=LcT[:, hq, :], in_=ps[b6][:, hq * 128:(hq + 1) * 128], func=AF.Exp,
                                     bias=gnegd[:, tt, h0 + hq:h0 + hq + 1], reads=["gnegd"], writes=[PK(b6), ("LcT", gr)])
                            yield
                            b7 = nsm()
                            for hq in range(NH):
                                P.op("pe", "matmul", ps[b7][:, hq * 128:(hq + 1) * 128], lhsT=knT[:, hq, :], rhs=qTb[:, h0 + hq, tsl], start=True, stop=True,
                                     reads=[("knT", gr), ("qTb", h0 + hq)], writes=[PK(b7)])
                            P.op("dve", "tensor_tensor", out=QKT[:].rearrange("p h j -> p (h j)"), in0=ps[b7][:, 0:128 * NH],
                                 in1=LcT[:].rearrange("p h j -> p (h j)"), op=ALU.mult, reads=[("LcT", gr)], writes=[PK(b7), ("QKT", gr)])
                            b8 = nsm()
                            for hq in range(NH):
                                P.op("pe", "matmul", ps[b8][:, hq * 128:(hq + 1) * 128], lhsT=kbd[:, hq, :], rhs=Pb[:, hq, :], start=True, stop=True,
                                     reads=[("kbd", gr), ("Pb", gr)], writes=[PK(b8)])
                            wTn = sqs[:, 0:NH, :]
                            vnew = sqs[:, NH:2 * NH, :]
                            P.op("act", "activation", out=wTn, in_=ps[b8][:, 0:128 * NH].rearrange("p (h j) -> p h j", h=NH), func=AF.Copy, scale=-1.0,
                                 writes=[PK(b8), ("sqs", gr)])
                            yield
                            b9 = nsm()
                            for hq in range(NH):
                                o9 = ps[b9][:, hq * 128:(hq + 1) * 128]
                                P.op("pe", "matmul", o9, lhsT=Pb[:, hq, :], rhs=vb[:, hq, :], start=True, stop=False, reads=[("Pb", gr), ("vb", gr)], writes=[PK(b9)])
                                P.op("pe", "matmul", o9, lhsT=wTn[:, hq, :], rhs=Sb[:, h0 + hq, :], start=False, stop=True,
                                     reads=[("sqs", gr), ("Sb", h0 + hq)], writes=[PK(b9)])
                            P.op("act", "copy", out=vnew, in_=ps[b9][:, 0:128 * NH].rearrange("p (h j) -> p h j", h=NH), reads=[("sqs", gr)], writes=[PK(b9), ("sqs", gr)])
                            yield
                            b10 = nsm()
                            b11 = nsm()
                            for hq in range(NH):
                                P.op("pe", "matmul", ps[b10][:, hq * 128:(hq + 1) * 128], lhsT=qTb[:, h0 + hq, tsl], rhs=Sb[:, h0 + hq, :], start=True, stop=True,
                                     reads=[("qTb", h0 + hq), ("Sb", h0 + hq)], writes=[PK(b10)])
                            for hq in range(NH):
                                P.op("pe", "matmul", ps[b11][:, hq * 128:(hq + 1) * 128], lhsT=QKT[:, hq, :], rhs=vnew[:, hq, :], start=True, stop=True,
                                     reads=[("QKT", gr), ("sqs", gr)], writes=[PK(b11)])
                            P.op("dve", "tensor_tensor", out=LbT[:], in0=ps[b10][:, 0:128 * NH].rearrange("p (h j) -> p h j", h=NH),
                                 in1=gEd[:, tt, hsl].unsqueeze(2).to_broadcast([128, NH, 128]), op=ALU.mult,
                                 reads=["gEd"], writes=[PK(b10), ("LbT", gr)])
                            osbf = osb[:].rearrange("p h j -> p (h j)")
                            P.op("dve", "tensor_tensor", out=osbf, in0=LbTf, in1=ps[b11][:, 0:128 * NH], op=ALU.add,
                                 reads=[("LbT", gr)], writes=[PK(b11), ("LbT", gr)])
                            yield
                            b12 = nsm()
                            for hq in range(NH):
                                P.op("pe", "matmul", ps[b12][:, hq * 128:(hq + 1) * 128], lhsT=kdec[:, hq, :], rhs=vnew[:, hq, :], start=True, stop=True,
                                     reads=[("kdec", gr), ("sqs", gr)], writes=[PK(b12)])
                            SK_ = [("S", l, h0 + q) for q in range(NH)]
                            SBK_ = [("Sb", h0 + q) for q in range(NH)]
                            P.op("dve", "tensor_tensor", out=Sst[:, hsl, :], in0=Sst[:, hsl, :], in1=gEl[:, tt, hsl].unsqueeze(2).to_broadcast([128, NH, 128]),
                                 op=ALU.mult, reads=SK_ + ["gEl"], writes=SK_)
                            P.op("dve", "tensor_tensor", out=Sst[:, hsl, :], in0=Sst[:, hsl, :], in1=ps[b12][:, 0:128 * NH].rearrange("p (h j) -> p h j", h=NH),
                                 op=ALU.add, reads=SK_, writes=[PK(b12)] + SK_)
                            P.op("act", "copy", out=Sb[:, hsl, :], in_=Sst[:, hsl, :], reads=SK_, writes=SBK_)
                            yield
                            for hq in range(NH):
                                P.op("act", "activation", out=kn_tm[:, hq, :], in_=osb[:, hq, :], func=AF.Square, accum_out=st3[:, 28 + hq:29 + hq],
                                     reads=[("LbT", gr)], writes=[("kn_tm", gr), ("st3_o", gr, hq)])
                            SO_ = [("st3_o", gr, q) for q in range(NH)]
                            rq = st3[:, 8 + NH:8 + 2 * NH]
                            P.op("dve", "scalar_tensor_tensor", out=st3[:, 32:32 + NH], in0=rq, scalar=1.0 / 16384.0, in1=rq, op0=ALU.mult, op1=ALU.mult,
                                 reads=[("st3_r", gr)], writes=[("st3_r2", gr)])
                            P.op("dve", "tensor_tensor", out=st3[:, 36:36 + NH], in0=st3[:, 28:28 + NH], in1=st3[:, 32:32 + NH], op=ALU.mult,
                                 reads=SO_ + [("st3_r2", gr)], writes=[("st3_m", gr)])
                            P.op("pool", "tensor_tensor", out=st3[:, 36:36 + NH], in0=st3[:, 36:36 + NH], in1=epsc.to_broadcast([128, NH]), op=ALU.add,
                                 reads=[("st3_m", gr), "konst"], writes=[("st3_m", gr)])
                            rsqrt_cols(st3[:, 40:40 + NH], st3[:, 36:36 + NH], NH, [("st3_m", gr)], [("st3_rs", gr)])
                            P.op("dve", "scalar_tensor_tensor", out=st3[:, 44:44 + NH], in0=st3[:, 40:40 + NH], scalar=128.0 ** -0.5, in1=rq,
                                 op0=ALU.mult, op1=ALU.mult, reads=[("st3_rs", gr), ("st3_r", gr)], writes=[("st3_f", gr)])
                            P.op("dve", "tensor_tensor", out=osb[:], in0=osb[:], in1=st3[:, 44:44 + NH].unsqueeze(2).to_broadcast([128, NH, 128]), op=ALU.mult,
                                 reads=[("LbT", gr), ("st3_f", gr)], writes=[("LbT", gr)])
                            P.op("dve", "tensor_tensor", out=kn_tm[:], in0=osb[:], in1=Gz[:, hsl, tt, :], op=ALU.mult,
                                 reads=[("LbT", gr)] + GZ_, writes=[("kn_tm", gr)])
                            yield
                            b13 = nsm()
                            v13 = ps[b13][:, 0:64 * NH].bitcast(BF16).rearrange("p (h j) -> p h j", h=NH)
                            for hq in range(NH):
                                P.op("pe", "transpose", v13[:, hq, :], kn_tm[:, hq, :], identB[:], reads=[("kn_tm", gr), "identB"], writes=[PK(b13)])
                            P.op("act", "copy", out=yT[:, 8 + h0:8 + h0 + NH, tsl], in_=v13,
                                 writes=[PK(b13)] + [("yT", 8 + h0 + q, tt) for q in range(NH)])
                            yield

                def outproj_half(kh):
                    wo = wout_d[l].rearrange("(k p) e -> p k e", p=128)
                    for ch in range(2):
                        sl, Wo = load_M(wo[:, kh * 8:(kh + 1) * 8, ch * 512:(ch + 1) * 512], (8, 512))
                        for tt in range(ST):
                            tsl = slice(tt * 128, (tt + 1) * 128)
                            bb = nbig()
                            for kc in range(8):
                                P.op("pe", "matmul", ps[bb][:, :], lhsT=yT[:, kh * 8 + kc, tsl], rhs=Wo[:, kc, :],
                                     start=(kc == 0), stop=(kc == 7),
                                     reads=[("yT", kh * 8 + c, tt) for c in range(8)] + [("ringM", sl)], writes=[PK(bb)])
                            P.op("dve", "tensor_tensor", out=H[:, tt, ch * 512:(ch + 1) * 512], in0=H[:, tt, ch * 512:(ch + 1) * 512],
                                 in1=ps[bb][:, :], op=ALU.add, reads=[("H", tt)], writes=[PK(bb), ("H", tt)])
                            yield

                def gmlp_stream():
                    wv = win_d[l].rearrange("(k p) e -> p k e", p=128)
                    for ch in range(2):
                        sl, Wm = load_M(wv[:, :, ch * 512:(ch + 1) * 512], (8, 512))
                        for blk in range(4):
                            c = ch * 4 + blk
                            bb = nbig()
                            for kc in range(8):
                                P.op("pe", "matmul", ps[bb][:, :], lhsT=Wm[:, kc, blk * 128:(blk + 1) * 128], rhs=xnT[:, kc, :], start=(kc == 0), stop=(kc == 7),
                                    reads=XT + [("ringM", sl)], writes=[PK(bb)])
                            P.op("act", "activation", out=gu[:, c, :], in_=ps[bb][:, :], func=AF.Gelu_apprx_tanh,
                                 writes=[PK(bb), ("gu", c)])
                            yield
                    for ch in range(2):
                        sl, Wm = load_M(wv[:, :, 2048 + ch * 512:2048 + (ch + 1) * 512], (8, 512))
                        for blk in range(4):
                            c = ch * 4 + blk
                            bb = nbig()
                            for kc in range(8):
                                P.op("pe", "matmul", ps[bb][:, :], lhsT=Wm[:, kc, blk * 128:(blk + 1) * 128], rhs=xnT[:, kc, :], start=(kc == 0), stop=(kc == 7),
                                    reads=XT + [("ringM", sl)], writes=[PK(bb)])
                            P.op("act", "activation", out=xn[0][:, 0:TS], in_=ps[bb][:, :], func=AF.Silu,
                                 writes=[PK(bb), ("xn", 0)])
                            P.op("dve", "tensor_tensor", out=gu[:, c, :], in0=gu[:, c, :], in1=xn[0][:, 0:TS], op=ALU.mult,
                                 reads=[("gu", c), ("xn", 0)], writes=[("gu", c)])
                            yield
                    sl0, Wv0 = load_M(wv[:, :, 1024:1536], (8, 512))
                    sl1, Wv1 = load_M(wv[:, :, 1536:2048], (8, 512))
                    gv = nln
                    for tt in range(ST):
                        tsl = slice(tt * 128, (tt + 1) * 128)
                        bbs = []
                        for half, (slh, Wh) in enumerate(((sl0, Wv0), (sl1, Wv1))):
                            bb = nbig()
                            bbs.append(bb)
                            for kc in range(8):
                                P.op("pe", "matmul", ps[bb][:, :], lhsT=xnT[:, kc, tsl], rhs=Wh[:, kc, :], start=(kc == 0), stop=(kc == 7),
                                    reads=[("xnT", tt), ("ringM", slh)], writes=[PK(bb)])
                            P.op("act", "activation", out=gv[:, half * 512:(half + 1) * 512], in_=ps[bb][:, :],
                                                                                  func=AF.Gelu_apprx_tanh, accum_out=stn[:, 4 + half:5 + half],
                                 writes=[PK(bb), "nln", ("stn4", half)])
                            yield
                        P.op("dve", "tensor_tensor", out=stn[:, 6:7], in0=stn[:, 4:5], in1=stn[:, 5:6], op=ALU.add,
                             reads=[("stn4", 0), ("stn4", 1)], writes=["stn6"])
                        P.op("dve", "tensor_scalar", out=stn[:, 7:8], in0=stn[:, 6:7], scalar1=-1.0 / D_MODEL, scalar2=None, op0=ALU.mult,
                             reads=["stn6"], writes=["stn7"])
                        P.op("act", "activation", out=xn[0][:], in_=gv[:], func=AF.Square, bias=stn[:, 7:8], accum_out=stn[:, 8:9],
                             reads=["nln", "stn7"], writes=[("xn", 0), "stn8"])
                        P.op("dve", "tensor_scalar", out=stn[:, 9:10], in0=stn[:, 8:9], scalar1=1.0 / D_MODEL, scalar2=EPS,
                                                              op0=ALU.mult, op1=ALU.add, reads=["stn8"], writes=["stn9"])
                        rsqrt_cols(stn[:, 10:11], stn[:, 9:10], 1, ["stn9"], ["stn10"])
                        P.op("dve", "tensor_tensor", out=stn[:, 11:12], in0=stn[:, 7:8], in1=stn[:, 10:11], op=ALU.mult,
                             reads=["stn7", "stn10"], writes=["stn11"])
                        P.op("act", "activation", out=nln[:], in_=gv[:], func=AF.Identity, bias=stn[:, 11:12], scale=stn[:, 10:11],
                             reads=["nln", "stn10", "stn11"], writes=["nln"])
                        for half in range(2):
                            bb = nbig()
                            for gq in range(4):
                                g = half * 4 + gq
                                P.op("pe", "matmul", ps[bb][:, gq * 128:(gq + 1) * 128], lhsT=nln[:, g * 128:(g + 1) * 128],
                                                                                 rhs=wsTb[:, g, :], start=True, stop=True,
                                     reads=["nln", "wsTb"], writes=[PK(bb)])
                            psv = ps[bb][:, :].rearrange("p (g i) -> p g i", g=4)
                            P.op("dve", "tensor_tensor", out=tmix[:], in0=psv, in1=cfm[:, C_LNG + half * 4:C_LNG + half * 4 + 4].unsqueeze(2).to_broadcast([128, 4, 128]),
                                op=ALU.mult, reads=["cfm"], writes=[PK(bb), "tmix"])
                            P.op("dve", "tensor_tensor", out=tmix[:], in0=tmix[:], in1=Rt[:, half * 4:half * 4 + 4, :], op=ALU.add,
                                 reads=["tmix", "Rt"], writes=["tmix"])
                            P.op("dve", "tensor_tensor", out=yT[:, half * 4:half * 4 + 4, tsl], in0=tmix[:],
                                                                                       in1=gu[:, half * 4:half * 4 + 4, tsl], op=ALU.mult,
                                 reads=["tmix"] + [("gu", half * 4 + q) for q in range(4)], writes=[("yT", half * 4 + q, tt) for q in range(4)])
                            yield
                    if "outproj" in stages:
                        yield from outproj_half(0)

                def pump(gens, ratio):
                    alive = [True] * len(gens)
                    while any(alive):
                        for gi, (g_, r_) in enumerate(zip(gens, ratio)):
                            if not alive[gi]:
                                continue
                            for _ in range(r_):
                                try:
                                    next(g_)
                                except StopIteration:
                                    alive[gi] = False
                                    break
                ms = gmlp_stream() if "gmlp" in stages else iter(())
                if "gdn" in stages:
                    g1 = gdn_phase1()
                    alive1 = True
                    while alive1:
                        for _ in range(2):
                            try:
                                next(g1)
                            except StopIteration:
                                alive1 = False
                                break
                        try:
                            next(ms)
                        except StopIteration:
                            pass
                    tg = [gdn_tiles(gr) for gr in range(8 // NH)]
                    ng_ = len(tg)
                    for gi_ in range(ng_ - 1):
                        for _ in range(GOFF * (ng_ - 1 - gi_)):
                            try:
                                next(tg[gi_])
                            except StopIteration:
                                break
                    pump(tg + [ms], [1] * (8 // NH) + [1])
                else:
                    pump([ms], [1])

                if "outproj" in stages:
                    for _ in outproj_half(1):
                        pass

                if "tail" in stages:
                    wpk, Wp = load_Gx(wple_d[l].rearrange("(k p) e -> p k e", p=128), (2, 1024))
                    wg = wgate_d[l].rearrange("(k p) e -> p k e", p=128)
                    slg, Wgs = [], []
                    for half in range(2):
                        sl_, W_ = load_M(wg[:, :, half * 512:(half + 1) * 512], (8, 512))
                        slg.append(sl_)
                        Wgs.append(W_)
                    for tt in range(ST):
                        norm_to_T(l, T0 + tt, tt, C_GNW, xnT)
                    for tt in range(ST):
                        T = T0 + tt
                        tsl = slice(tt * 128, (tt + 1) * 128)
                        pt = acc[0][:, 0:256]
                        P.dma("sp", ("ldp", l % 2), out=pt, in_=p_d[l, (s * ST + T) * 128:(s * ST + T + 1) * 128, :], writes=[("acc", 0)])
                        b = nsm()
                        for kc in range(2):
                            P.op("pe", "transpose", ps[b][:, kc * 128:(kc + 1) * 128], pt[:, kc * 128:(kc + 1) * 128], identF,
                                 reads=[("acc", 0), "konst"], writes=[PK(b)])
                        P.op("act", "copy", out=pT[:, :, tsl], in_=ps[b][:, 0:256].rearrange("p (k j) -> p k j", k=2),
                             writes=[PK(b), ("xn", 0)])
                        ef = TM1
                        for half in range(2):
                            bb = nbig()
                            for kc in range(2):
                                P.op("pe", "matmul", ps[bb][:, :], lhsT=pT[:, kc, tsl], rhs=Wp[:, kc, half * 512:(half + 1) * 512],
                                     start=(kc == 0), stop=(kc == 1), reads=[("xn", 0)] + wpk, writes=[PK(bb)])
                            P.op("act", "copy", out=ef[:, half * 512:(half + 1) * 512], in_=ps[bb][:, :],
                                 writes=[PK(bb), ("TM", 1, half)])
                        P.op("act", "activation", out=nln[:], in_=ef[:], func=AF.Square, accum_out=stn[:, 12:13],
                             reads=[("TM", 1, 0), ("TM", 1, 1)], writes=["nln", "stn12"])
                        P.op("dve", "tensor_scalar", out=stn[:, 13:14], in0=stn[:, 12:13], scalar1=1.0 / D_MODEL, scalar2=EPS,
                             op0=ALU.mult, op1=ALU.add, reads=["stn12"], writes=["stn13"])
                        rsqrt_cols(stn[:, 14:15], stn[:, 13:14], 1, ["stn13"], ["stn14"])
                        P.op("dve", "scalar_tensor_tensor", out=ef[:], in0=ef[:], scalar=stn[:, 14:15], in1=cbc[:, B_PNW:B_PNW + 1024],
                             op0=ALU.mult, op1=ALU.mult,
                             reads=[("TM", 1, 0), ("TM", 1, 1), "stn14", "cbc"], writes=[("TM", 1, 0), ("TM", 1, 1)])
                        for half in range(2):
                            bb = nbig()
                            for kc in range(8):
                                P.op("pe", "matmul", ps[bb][:, :], lhsT=xnT[:, kc, tsl], rhs=Wgs[half][:, kc, :],
                                     start=(kc == 0), stop=(kc == 7), reads=[("xnT", tt), ("ringM", slg[half])], writes=[PK(bb)])
                            P.op("act", "activation", out=gsig[:], in_=ps[bb][:, :], func=AF.Sigmoid, writes=[PK(bb), "tmix"])
                            P.op("dve", "tensor_tensor", out=gsig[:], in0=gsig[:], in1=ef[:, half * 512:(half + 1) * 512], op=ALU.mult,
                                 reads=["tmix", ("TM", 1, half)], writes=["tmix"])
                            P.op("dve", "tensor_tensor", out=H[:, T, half * 512:(half + 1) * 512], in0=H[:, T, half * 512:(half + 1) * 512],
                                 in1=gsig[:], op=ALU.add, reads=[("H", T), "tmix"], writes=[("H", T)])

            P.dma("sp", "c_cbc", out=cbc[:, 0:1024], in_=fnw_d[:, :], writes=["cbc"])
            for T in range(ST):
                ob = TM1
                P.op("act", "activation", out=nln[:], in_=H[:, T, :], func=AF.Square, accum_out=stn[:, 0:1],
                     reads=[("H", T)], writes=["nln", "stn0"])
                P.op("dve", "tensor_scalar", out=stn[:, 1:2], in0=stn[:, 0:1], scalar1=1.0 / D_MODEL, scalar2=EPS,
                     op0=ALU.mult, op1=ALU.add, reads=["stn0"], writes=["stn1"])
                rsqrt_cols(stn[:, 2:3], stn[:, 1:2], 1, ["stn1"], ["stn2"])
                P.op("dve", "scalar_tensor_tensor", out=ob[:], in0=H[:, T, :], scalar=stn[:, 2:3], in1=cbc[:, 0:1024],
                     op0=ALU.mult, op1=ALU.mult, reads=[("H", T), "stn2", "cbc"], writes=[("TM", 1, 0), ("TM", 1, 1)])
                Tg = s * ST + T
                outs.append(P.dma("sp", ("sto", 0), out=out_d[Tg * 128:(Tg + 1) * 128, :], in_=ob[:],
                                  reads=[("TM", 1, 0), ("TM", 1, 1)]))
        P.emit(final_waits=outs)
        nops = len(P.ops)
    return nc, nops


def _konst():
    k = np.zeros((128, NK), np.float32)
    k[:, K_ID:K_ID + 128] = np.eye(128, dtype=np.float32)
    p = np.arange(128)[:, None]
    i = np.arange(128)[None, :]
    k[:, K_U:K_U + 128] = (p <= i)
    k[:, K_NC:K_NC + 128] = np.where(p <= i, 0.0, NEG)
    k[:, K_NS:K_NS + 128] = np.where(p < i, 0.0, NEG)
    k[:, K_ONE:K_ONE + 128] = 1.0
    for h in range(8):
        k[h, K_ES + h * 128:K_ES + (h + 1) * 128] = 1.0
    k[:, K_MH] = -0.5
    k[:, K_EPS] = EPS
    return k


def _layout(inp, layers):
    f = lambda a: np.ascontiguousarray(np.asarray(a, dtype=np.float32))
    Ls = list(layers)
    n = len(Ls)
    cfm = np.zeros((n, 128, NFM), np.float32)
    cbc = np.zeros((n, 128, NBC), np.float32)
    bsp = np.zeros((n, 128, 1024), np.float32)
    wst = np.zeros((n, 128, 8, 128), np.float32)
    for a, l in enumerate(Ls):
        cfm[a, :, C_NW:C_NW + 8] = f(inp["norm_w"][l]).reshape(8, 128).T
        cfm[a, :, C_GNW:C_GNW + 8] = f(inp["ple_gate_norm_w"][l]).reshape(8, 128).T
        cfm[a, :, C_LNG:C_LNG + 8] = f(inp["ln_v_g"][l]).reshape(8, 128).T
        cfm[a, :, C_LNB:C_LNB + 8] = f(inp["ln_v_b"][l]).reshape(8, 128).T
        cw = f(inp["conv_w"][l]).reshape(4, 24, 128)
        cfm[a, :, C_CONV:C_CONV + 96] = cw.transpose(2, 0, 1).reshape(128, 96)
        cbc[a, :, B_PNW:B_PNW + 1024] = f(inp["ple_norm_w"][l])[None, :]
        cbc[a, :, B_GDN:B_GDN + 128] = f(inp["gdn_norm_w"][l])[None, :]
        cbc[a, :, B_DTB:B_DTB + 8] = f(inp["dt_bias"][l])[None, :]
        cbc[a, :, B_ALOG:B_ALOG + 8] = f(inp["A_log"][l])[None, :]
        bsp[a] = f(inp["b_spatial"][l]).reshape(1, 1024)
        wst[a] = f(inp["w_spatial"][l]).transpose(2, 0, 1)
    return cfm, cbc, bsp, wst


_CACHE = {}


def _get(L, emit_h):
    key = (L, emit_h)
    if key not in _CACHE:
        _CACHE[key] = build(L, emit_h=emit_h)[0]
    return _CACHE[key]


FUSED = True


def kernel(x, p, norm_w, w_in, ln_v_g, ln_v_b, w_spatial, b_spatial, conv_w, A_log, dt_bias,
           gdn_norm_w, w_out, w_ple, ple_norm_w, ple_gate_norm_w, w_ple_gate, final_norm_w):
    inp = dict(norm_w=norm_w, ln_v_g=ln_v_g, ln_v_b=ln_v_b, w_spatial=w_spatial, b_spatial=b_spatial, conv_w=conv_w,
               A_log=A_log, dt_bias=dt_bias, gdn_norm_w=gdn_norm_w, ple_norm_w=ple_norm_w, ple_gate_norm_w=ple_gate_norm_w)
    f = lambda a: np.ascontiguousarray(np.asarray(a, dtype=np.float32))
    x = f(x)
    p = f(p)
    w_in, w_out, w_ple, w_gate = f(w_in), f(w_out), f(w_ple), f(w_ple_gate)
    fnw = np.ascontiguousarray(np.broadcast_to(f(final_norm_w)[None, :], (128, D_MODEL)))
    konst = _konst()
    depth = w_in.shape[0]
    nb = x.shape[0]
    if FUSED:
        cfm, cbc, bsp, wst = _layout(inp, range(depth))
        nc = _get(depth, False)
        maps = [dict(x=x[b], p=np.ascontiguousarray(p[:, b]), w_in=w_in, w_out=w_out, w_ple=w_ple, w_gate=w_gate,
                     wsT=wst, bsp=bsp, cfm=cfm, cbc=cbc, fnw=fnw, konst=konst) for b in range(nb)]
        res = run_bass_kernel_spmd(nc, maps, core_ids=list(range(nb)))
        return np.stack([np.asarray(r["out"]) for r in res.results], axis=0).astype(np.float32)
    raise RuntimeError("unfused path removed")
```

```python
import numpy as np
from contextlib import ExitStack
import concourse.bass as bass
import concourse.mybir as mybir
from concourse.bass_utils import run_bass_kernel_spmd

F32 = mybir.dt.float32
BF16 = mybir.dt.bfloat16
AF = mybir.ActivationFunctionType
ALU = mybir.AluOpType

COMPUTE = ("pe", "act", "dve", "pool")

D_MODEL = 1024
SEQ = 2048
NT = 16
ST = 4
NH = 4
NSUP = NT // ST
TS = ST * 128
IN_DIM = 7184
EPS = 1e-6
NEG = -65536.0
GOFF = 11
NBIG = 3
CH_DT = mybir.dt.float32r

K_ID = 0
K_U = 128
K_ONE = 256
K_ES = 384
K_MH = 384 + 1024
K_EPS = K_MH + 1
NKS = K_EPS + 1
K_NC = NKS
K_NS = NKS + 128
NK = NKS + 256
C_NW, C_GNW, C_LNG, C_LNB, C_CONV = 0, 8, 16, 24, 32
NFM = 128
B_PNW, B_GDN, B_DTB, B_ALOG = 0, 1024, 1152, 1160
NBC = 1168


class _Op:
    __slots__ = ("eng", "emit", "reads", "writes", "dma_sem", "idx", "pos", "deps",
                 "need_inc", "tick", "dma_cnt", "know")

    def __init__(self, eng, emit, reads, writes, dma_sem):
        self.eng, self.emit, self.reads, self.writes, self.dma_sem = eng, emit, reads, writes, dma_sem
        self.deps = []
        self.need_inc = False
        self.tick = None
        self.dma_cnt = None
        self.know = None


class Prog:
    def __init__(self, nc, epoch=2000):
        self.nc = nc
        self.ops = []
        self.epoch = epoch

    def op(self, eng, fname, *args, reads=(), writes=(), **kw):
        self.ops.append(_Op(eng, (fname, args, kw), tuple(reads), tuple(writes), None))

    def dma(self, queue, sem, reads=(), writes=(), **kw):
        o = _Op(queue, ("dma_start", (), kw), tuple(reads), tuple(writes), sem)
        self.ops.append(o)
        return o

    def analyze(self):
        last_w, readers, pos_ctr, dma_ctr, know = {}, {}, {}, {}, {}
        for i, o in enumerate(self.ops):
            o.idx = i
            if o.dma_sem is not None:
                src = ("dma", o.dma_sem)
                dma_ctr[src] = dma_ctr.get(src, 0) + 1
                o.pos = dma_ctr[src]
            else:
                pos_ctr[o.eng] = pos_ctr.get(o.eng, 0) + 1
                o.pos = pos_ctr[o.eng]

        def src_of(o):
            return ("dma", o.dma_sem) if o.dma_sem is not None else o.eng

        ops = self.ops
        for o in ops:
            cand = set()
            raw = set()
            for k in o.reads:
                w = last_w.get(k)
                if w is not None:
                    cand.add(w)
                    raw.add(w)
            for k in o.writes:
                w = last_w.get(k)
                if w is not None:
                    cand.add(w)
                for r in readers.get(k, ()):
                    cand.add(r)
            ek = know.setdefault(o.eng, {})
            best = {}
            for p in cand:
                po = ops[p]
                if po is o:
                    continue
                if po.dma_sem is None and o.dma_sem is None and po.eng == o.eng:
                    if o.eng == "pe":
                        continue
                s = src_of(po)
                if po.pos <= ek.get(s, 0):
                    continue
                if s not in best or ops[best[s]].pos < po.pos:
                    best[s] = p
            final = []
            for s, p in sorted(best.items(), key=lambda kv: -kv[1]):
                po = ops[p]
                if po.pos <= ek.get(s, 0):
                    continue
                final.append(p)
                po.need_inc = True
                ek[s] = po.pos
                for s2, v2 in po.know.items():
                    if ek.get(s2, 0) < v2:
                        ek[s2] = v2
            o.deps = final
            o.know = dict(ek)
            for k in o.reads:
                readers.setdefault(k, []).append(o.idx)
            for k in o.writes:
                last_w[k] = o.idx
                readers[k] = []
        tick = {}
        for o in ops:
            if o.dma_sem is not None:
                o.dma_cnt = 16 * o.pos
            elif o.need_inc:
                tick[o.eng] = tick.get(o.eng, 0) + 1
                o.tick = tick[o.eng]
        self.nticks = tick
        self.dma_sems = sorted({o.dma_sem for o in ops if o.dma_sem is not None}, key=str)

    def emit(self, final_waits=()):
        nc = self.nc
        self.analyze()
        ops = self.ops
        with ExitStack() as st:
            esems = {}
            for e in COMPUTE:
                n = self.nticks.get(e, 0)
                ne = (n + self.epoch - 1) // self.epoch
                esems[e] = [st.enter_context(nc.semaphore(f"s_{e}_{j}")) for j in range(ne)]
            dsems = {k: st.enter_context(nc.semaphore(f"d_{i}")) for i, k in enumerate(self.dma_sems)}
            block = st.enter_context(nc.Block())

            def sem_for(po):
                if po.dma_sem is not None:
                    return dsems[po.dma_sem], po.dma_cnt
                t = po.tick - 1
                return esems[po.eng][t // self.epoch], (t % self.epoch) + 1

            def run(engname, engobj):
                for o in ops:
                    if o.eng != engname:
                        continue
                    for p in o.deps:
                        s, v = sem_for(ops[p])
                        engobj.wait_ge(s, v)
                    fn, a, kw = o.emit
                    ins = getattr(engobj, fn)(*a, **kw)
                    if o.dma_sem is not None:
                        ins.then_inc(dsems[o.dma_sem], 16)
                    elif o.need_inc:
                        s, _ = sem_for(o)
                        ins.then_inc(s, 1)
                if engname == "sp":
                    for o in final_waits:
                        s, v = sem_for(o)
                        engobj.wait_ge(s, v)

            @block.tensor
            def _(e):
                run("pe", e)

            @block.scalar
            def _(e):
                run("act", e)

            @block.vector
            def _(e):
                run("dve", e)

            @block.gpsimd
            def _(e):
                run("pool", e)

            @block.sync
            def _(e):
                run("sp", e)


ALL_STAGES = ("consts", "A", "prep", "gdn", "gmlp", "outproj", "tail")


def build(L, emit_h=False, chain_dt=F32, stages=ALL_STAGES, nsup=NSUP, gdn_heads=8, gdn_cut=99):
    nc = bass.Bass("TRN2", target_bir_lowering=False)

    def din(name, shape):
        return nc.dram_tensor(name, list(shape), F32, kind="ExternalInput").ap()

    x_d = din("x", [SEQ, D_MODEL])
    p_d = din("p", [L, SEQ, 256])
    win_d = din("w_in", [L, D_MODEL, IN_DIM])
    wout_d = din("w_out", [L, 2048, D_MODEL])
    wple_d = din("w_ple", [L, 256, D_MODEL])
    wgate_d = din("w_gate", [L, D_MODEL, D_MODEL])
    wst_d = din("wsT", [L, 128, 8, 128])
    bsp_d = din("bsp", [L, 128, 1024])
    cfm_d = din("cfm", [L, 128, NFM])
    cbc_d = din("cbc", [L, 128, NBC])
    fnw_d = din("fnw", [128, D_MODEL])
    konst_d = din("konst", [128, NK])
    out_d = nc.dram_tensor("out", [SEQ, D_MODEL], F32, kind="ExternalOutput").ap()
    hout_d = nc.dram_tensor("hout", [SEQ, D_MODEL], F32, kind="ExternalOutput").ap() if emit_h else None

    with ExitStack() as st:
        def sb(name, shape, dt=F32):
            return st.enter_context(nc.sbuf_tensor("sb_" + name, list(shape), dt))

        H = sb("H", [128, ST, D_MODEL])
        konst = sb("konst", [128, NKS])
        identB = sb("identB", [128, 128], BF16)
        negcB = sb("negcB", [128, 128], BF16)
        negsB = sb("negsB", [128, 128], BF16)
        onesB = sb("onesB", [128, 128], BF16)
        cfm = sb("cfm", [128, NFM])
        cbc = sb("cbc", [128, NBC])
        wab = sb("wab", [128, 8, 16], BF16)
        wsTb = sb("wsTb", [128, 8, 128], BF16)
        Rt = sb("Rt", [128, 8, 128], BF16)
        negA = sb("negA", [128, 8])
        xn = [sb(f"xn{i}", [128, D_MODEL], BF16) for i in range(1)]
        xnT = sb("xnT", [128, 8, TS], BF16)
        ringG = [sb(f"ringG{i}", [128, 4096], BF16) for i in range(2)]
        ringM = [sb(f"ringM{i}", [128, 4096], BF16) for i in range(2)]
        gu = sb("gu", [128, 8, TS], BF16)
        TM1 = sb("TM1", [128, D_MODEL])
        nln = sb("nln", [128, D_MODEL], BF16)
        tmix = sb("tmix", [128, 4, 128])
        gsig = tmix[:].rearrange("p g i -> p (g i)")
        yT = sb("yT", [128, 16, TS], BF16)
        pre = [sb(f"pre{i}", [128, TS + 3]) for i in range(1)]
        acc = [sb(f"acc{i}", [128, TS]) for i in range(1)]
        qTb = sb("qTb", [128, 8, TS], BF16)
        kTb = sb("kTb", [128, 8, TS], BF16)
        vTb = sb("vTb", [128, 8, TS], BF16)
        halo = sb("halo", [128, L * 24, 3])
        Gz = sb("Gz", [128, 8, ST, 128], BF16)
        rst = sb("rst", [128, 2, ST, 8])
        rr = sb("rr", [128, 2, ST, 8])
        gx1 = sb("gx1", [128, ST, 8])
        gbs = sb("gbs", [128, ST, 8])
        ge1 = sb("ge1", [128, ST, 8])
        gsp = sb("gsp", [128, ST, 8])
        gg = sb("gg", [128, ST, 8])
        glb = sb("glb", [128, ST, 8])
        glnb = sb("glnb", [128, ST, 8])
        gbeta = sb("gbeta", [128, ST, 8])
        gd = sb("gd", [128, ST, 8])
        gnegd = sb("gnegd", [128, ST, 8])
        gEd = sb("gEd", [128, ST, 8])
        gtdl = sb("gtdl", [128, ST, 8])
        gEl = sb("gEl", [128, ST, 8])
        GP = sb("GP", [128, ST, 8, 3])
        class _NS:
            pass
        GT = []
        for gi in range(8 // NH):
            t_ = _NS()
            t_.dT3 = sb(f"dT3_{gi}", [8, 256])
            t_.sqs = sb(f"sqs_{gi}", [128, 2 * NH, 128], BF16)
            for nm in ("kn_tm", "kbd", "kdec", "vb", "knT", "Pb", "QKT", "LcT"):
                setattr(t_, nm, sb(f"{nm}_{gi}", [128, NH, 128], BF16))
            t_.LbT = sb(f"LbT_{gi}", [128, NH, 128])
            t_.Bc = sb(f"Bc_{gi}", [128, NH, 128], CH_DT)
            t_.Ac = sb(f"Ac_{gi}", [128, NH, 128], CH_DT)
            t_.Pc = sb(f"Pc_{gi}", [128, NH, 128], CH_DT)
            t_.osb = t_.LbT
            t_.st3 = sb(f"st3_{gi}", [128, 48])
            GT.append(t_)
        sqs = GT[0].sqs
        stn = sb("stn", [128, 16])
        SstL = [sb(f"Sst{i}", [128, 8, 128]) for i in range(L)]
        Sb = sb("Sb", [128, 8, 128], BF16)
        pT = xn[0][:].rearrange("p (k j) -> p k j", k=2)

        ps = [st.enter_context(nc.psum_tensor(f"ps{i}", [128, 512], F32)) for i in range(8)]

        P = Prog(nc)
        big_ctr = [0]
        small_ctr = [0]

        def nbig():
            b = big_ctr[0] % NBIG
            big_ctr[0] += 1
            return b

        def nsm():
            b = NBIG + small_ctr[0] % (8 - NBIG)
            small_ctr[0] += 1
            return b

        def PK(b):
            return ("ps", b)

        identF = konst[:, K_ID:K_ID + 128]
        Umat = konst[:, K_U:K_U + 128]
        onesF = konst[:, K_ONE:K_ONE + 128]
        mhalf = konst[:, K_MH:K_MH + 1]
        epsc = konst[:, K_EPS:K_EPS + 1]

        def mmv(ap):
            return ap

        def esel(h):
            return konst[0:8, K_ES + h * 128:K_ES + (h + 1) * 128]

        def rsqrt_cols(dst, src, n, rkeys, wkeys):
            P.op("pool", "tensor_tensor", out=dst, in0=src, in1=mhalf.to_broadcast([128, n]), op=ALU.pow,
                 reads=list(rkeys) + ["konst"], writes=wkeys)

        P.dma("sp", "c_konst", out=konst[:], in_=konst_d[:, 0:NKS], writes=["konst"])
        P.op("dve", "tensor_copy", out=identB[:], in_=identF, reads=["konst"], writes=["identB"])
        P.dma("pool", "c_negc", out=negcB[:], in_=konst_d[:, K_NC:K_NC + 128], writes=["negcB"])
        P.dma("pool", "c_negs", out=negsB[:], in_=konst_d[:, K_NS:K_NS + 128], writes=["negsB"])
        P.op("dve", "tensor_copy", out=onesB[:], in_=onesF, reads=["konst"], writes=["onesB"])
        P.op("dve", "memset", GP[:], 1.0, writes=["GP"])
        xv = x_d.rearrange("(t p) d -> p t d", p=128)
        for li in range(L):
            P.op("dve", "memset", SstL[li][:], 0.0, writes=[("S", li, h) for h in range(8)])
        P.op("dve", "memset", halo[:], 0.0, writes=[("halo", b_) for b_ in range(L * 24)])

        gslot = [0]
        mslot = [0]
        cur_l = [0]
        outs = []

        def load_G(l, hd):
            sl = gslot[0] % 2
            gslot[0] += 1
            dst = ringG[sl][:].rearrange("p (k c j) -> p k c j", k=8, c=4)
            src = win_d[l].rearrange("(k p) e -> p k e", p=128)[:, :, 3072:7168].rearrange(
                "p k (c h j) -> p k c h j", c=4, h=8)[:, :, :, hd, :]
            for c in range(4):
                P.dma("pool", ("rG", sl, c, l % 2), out=dst[:, :, c, :], in_=src[:, :, c, :], writes=[("ringG", sl, c)])
            return sl

        def load_Gx(src_ap, shape3):
            sl = gslot[0] % 2
            gslot[0] += 1
            k, c = shape3
            dst = ringG[sl][:, 0:k * c].rearrange("p (k c) -> p k c", k=k)
            keys = [("ringG", sl, q) for q in range(4)]
            P.dma("pool", ("rGx", sl, cur_l[0] % 2), out=dst, in_=src_ap, writes=keys)
            return keys, dst

        def load_M(src_ap, shape3):
            sl = mslot[0] % 2
            mslot[0] += 1
            k, c = shape3
            dst = ringM[sl][:, 0:k * c].rearrange("p (k c) -> p k c", k=k)
            P.dma("pool", ("rM", sl, cur_l[0] % 2), out=dst, in_=src_ap, writes=[("ringM", sl)])
            return sl, dst

        def norm_to_T(l, T, tt, col0, dstT):
            xb = xn[0]
            xk = ("xn", 0)
            P.op("act", "activation", out=xb[:], in_=H[:, T, :], func=AF.Square, accum_out=stn[:, 0:1],
                 reads=[("H", T)], writes=[xk, "stn0"])
            P.op("dve", "tensor_scalar", out=stn[:, 1:2], in0=stn[:, 0:1], scalar1=1.0 / D_MODEL, scalar2=EPS,
                                                  op0=ALU.mult, op1=ALU.add, reads=["stn0"], writes=["stn1"])
            rsqrt_cols(stn[:, 2:3], stn[:, 1:2], 1, ["stn1"], ["stn2"])
            P.op("act", "activation", out=xb[:], in_=H[:, T, :], func=AF.Copy, scale=stn[:, 2:3],
                 reads=[("H", T), "stn2"], writes=[xk])
            b = nsm()
            tpv = ps[b][:, :].bitcast(BF16).rearrange("p (k j) -> p k j", k=8)
            for kc in range(8):
                P.op("pe", "transpose", tpv[:, kc, :], xb[:, kc * 128:(kc + 1) * 128], identB[:],
                     reads=[xk, "identB"], writes=[PK(b)])
            P.op("dve", "tensor_tensor", out=dstT[:, :, tt * 128:(tt + 1) * 128], in0=tpv,
                                                  in1=cfm[:, col0:col0 + 8].unsqueeze(2).to_broadcast([128, 8, 128]),
                                                  op=ALU.mult,
                 reads=["cfm"], writes=[PK(b), ("xnT", tt)])

        for s in range(nsup):
            P.dma("sp", ("ldx", s % 2), out=H[:, :, :], in_=xv[:, s * ST:(s + 1) * ST, :], writes=[("H", t) for t in range(ST)])
            for l in range(L):
                cur_l[0] = l
                Sst = SstL[l]
                if "consts" in stages:
                    P.dma("sp", "c_cfm", out=cfm[:], in_=cfm_d[l], writes=["cfm"])
                    P.dma("sp", "c_cbc", out=cbc[:], in_=cbc_d[l], writes=["cbc"])
                    P.dma("pool", "c_rt", out=Rt[:].rearrange("p g i -> p (g i)"), in_=bsp_d[l], writes=["Rt"])
                    P.dma("pool", "c_wab", out=wab[:], in_=win_d[l].rearrange("(k p) e -> p k e", p=128)[:, :, 7168:7184], writes=["wab"])
                    P.dma("pool", "c_wst", out=wsTb[:], in_=wst_d[l], writes=["wsTb"])
                    P.op("pool", "memset", wsTb[64:128, :, 0:64], 0.0, reads=["wsTb"], writes=["wsTb"])
                    for half in range(2):
                        b = nbig()
                        for gq in range(4):
                            g = half * 4 + gq
                            P.op("pe", "matmul", ps[b][:, gq * 128:(gq + 1) * 128], lhsT=onesB[:],
                                                                             rhs=wsTb[:, g, :], start=True, stop=True,
                                 reads=["onesB", "wsTb"], writes=[PK(b)])
                        for gq in range(4):
                            g = half * 4 + gq
                            P.op("dve", "scalar_tensor_tensor", out=Rt[:, g, :], in0=ps[b][:, gq * 128:(gq + 1) * 128], scalar=cfm[:, C_LNB + g:C_LNB + g + 1],
                                in1=Rt[:, g, :], op0=ALU.mult, op1=ALU.add, reads=["cfm", "Rt"], writes=[PK(b), "Rt"])
                    P.op("act", "activation", out=negA[:], in_=cbc[:, B_ALOG:B_ALOG + 8], func=AF.Exp, reads=["cbc"], writes=["negA"])
                    P.op("dve", "tensor_scalar", out=negA[:], in0=negA[:], scalar1=-1.0, scalar2=None, op0=ALU.mult,
                         reads=["negA"], writes=["negA"])
                    P.op("act", "copy", out=Sb[:], in_=Sst[:], reads=[("S", l, h) for h in range(8)], writes=[("Sb", h) for h in range(8)])

                T0 = 0
                if "A" in stages:
                    for tt in range(ST):
                        norm_to_T(l, T0 + tt, tt, C_NW, xnT)
                XT = [("xnT", tt) for tt in range(ST)]

                if "prep" in stages:
                    b = nsm()
                    for tt in range(ST):
                        for kc in range(8):
                            P.op("pe", "matmul", ps[b][:, tt * 16:(tt + 1) * 16], lhsT=xnT[:, kc, tt * 128:(tt + 1) * 128], rhs=wab[:, kc, :],
                                start=(kc == 0), stop=(kc == 7), reads=[("xnT", tt), "wab"], writes=[PK(b)])
                    abv = ps[b][:, 0:ST * 16].rearrange("p (t c) -> p t c", c=16)
                    P.op("dve", "tensor_tensor", out=gx1[:], in0=abv[:, :, 0:8],
                                                          in1=cbc[:, B_DTB:B_DTB + 8].unsqueeze(1).to_broadcast([128, ST, 8]),
                                                          op=ALU.add, reads=["cbc"], writes=[PK(b), "gx1"])
                    P.op("dve", "tensor_copy", out=gbs[:], in_=abv[:, :, 8:16], writes=[PK(b), "gbs"])
                    P.op("act", "activation", out=ge1[:], in_=gx1[:], func=AF.Exp, reads=["gx1"], writes=["ge1"])
                    P.op("act", "activation", out=gsp[:], in_=ge1[:], func=AF.Ln, bias=1.0, reads=["ge1"], writes=["gsp"])
                    P.op("dve", "tensor_tensor", out=gg[:], in0=gsp[:], in1=negA[:].unsqueeze(1).to_broadcast([128, ST, 8]),
                                                          op=ALU.mult, reads=["gsp", "negA"], writes=["gg"])
                    P.op("act", "activation", out=ge1[:], in_=gbs[:], func=AF.Exp, scale=-1.0, reads=["gbs"], writes=["ge1"])
                    P.op("act", "activation", out=glb[:], in_=ge1[:], func=AF.Ln, bias=1.0, reads=["ge1"], writes=["glb"])
                    P.op("act", "activation", out=gbeta[:], in_=glb[:], func=AF.Exp, scale=-1.0, reads=["glb"], writes=["gbeta"])
                    P.op("dve", "tensor_scalar", out=glnb[:], in0=glb[:], scalar1=-1.0, scalar2=None, op0=ALU.mult,
                         reads=["glb"], writes=["glnb"])
                    bd = nsm()
                    bl = nsm()
                    for tt in range(ST):
                        P.op("pe", "matmul", ps[bd][:, tt * 8:(tt + 1) * 8], lhsT=Umat, rhs=gg[:, tt, :],
                                                                     start=True, stop=True, reads=["konst", "gg"], writes=[PK(bd)])
                    for tt in range(ST):
                        P.op("pe", "matmul", ps[bl][:, tt * 8:(tt + 1) * 8], lhsT=onesF, rhs=gg[:, tt, :],
                                                                     start=True, stop=True, reads=["konst", "gg"], writes=[PK(bl)])
                    gdf = gd[:].rearrange("p t h -> p (t h)")
                    P.op("dve", "tensor_copy", out=gdf, in_=ps[bd][:, 0:ST * 8], writes=[PK(bd), "gd"])
                    P.op("dve", "tensor_scalar", out=gnegd[:], in0=gd[:], scalar1=-1.0, scalar2=None, op0=ALU.mult,
                         reads=["gd"], writes=["gnegd"])
                    P.op("act", "activation", out=gEd[:], in_=gd[:], func=AF.Exp, reads=["gd"], writes=["gEd"])
                    P.op("dve", "tensor_tensor", out=gtdl[:].rearrange("p t h -> p (t h)"), in0=ps[bl][:, 0:ST * 8], in1=gdf,
                                                          op=ALU.subtract, reads=["gd"], writes=[PK(bl), "gtdl"])
                    P.op("act", "activation", out=gEl[:].rearrange("p t h -> p (t h)"), in_=ps[bl][:, 0:ST * 8], func=AF.Exp,
                         writes=[PK(bl), "gEl"])
                    P.op("act", "activation", out=GP[:, :, :, 2], in_=gtdl[:], func=AF.Exp, reads=["gtdl"], writes=["GP"])
                    P.op("dve", "tensor_tensor", out=GP[:, :, :, 1], in0=gbeta[:], in1=gEd[:], op=ALU.mult,
                         reads=["gbeta", "gEd", "GP"], writes=["GP"])

                def gdn_phase1():
                    nxt = load_G(l, 0)
                    for hd in range(8):
                        hq = hd
                        sl = nxt
                        if hd + 1 < 8:
                            nxt = load_G(l, hd + 1)
                        Wc = ringG[sl][:].rearrange("p (k c j) -> p k c j", k=8, c=4)
                        bz = nsm()
                        for tt in range(ST):
                            for kc in range(8):
                                P.op("pe", "matmul", ps[bz][:, tt * 128:(tt + 1) * 128], lhsT=xnT[:, kc, tt * 128:(tt + 1) * 128],
                                     rhs=Wc[:, kc, 3, :], start=(kc == 0), stop=(kc == 7),
                                     reads=[("xnT", tt), ("ringG", sl, 3)], writes=[PK(bz)])
                        P.op("act", "activation", out=Gz[:, hq, :, :].rearrange("p t j -> p (t j)"), in_=ps[bz][:, :], func=AF.Silu,
                             writes=[PK(bz), ("Gz", hq)])
                        P.op("dve", "tensor_tensor", out=Gz[:, hq, :, :], in0=Gz[:, hq, :, :],
                             in1=cbc[:, B_GDN:B_GDN + 128].unsqueeze(1).to_broadcast([128, ST, 128]), op=ALU.mult,
                             reads=[("Gz", hq), "cbc"], writes=[("Gz", hq)])
                        yield
                        for c3, dst, dk in ((0, qTb, "qTb"), (1, kTb, "kTb"), (2, vTb, "vTb")):
                            blk = c3 * 8 + hd
                            hb = l * 24 + blk
                            bb = nbig()
                            for kc in range(8):
                                P.op("pe", "matmul", ps[bb][:, :], lhsT=Wc[:, kc, c3, :], rhs=xnT[:, kc, :], start=(kc == 0), stop=(kc == 7),
                                     reads=XT + [("ringG", sl, c3)], writes=[PK(bb)])
                            pr, ac = pre[0], acc[0]
                            P.op("act", "copy", out=pr[:, 3:TS + 3], in_=ps[bb][:, :], writes=[PK(bb), ("pre", 0)])
                            P.op("dve", "tensor_copy", out=pr[:, 0:3], in_=halo[:, hb, :], reads=[("halo", hb)], writes=[("preh", 0)])
                            P.op("dve", "tensor_copy", out=halo[:, hb, :], in_=pr[:, TS:TS + 3], reads=[("pre", 0)], writes=[("halo", hb)])

                            def cw(tap, blk=blk):
                                c = C_CONV + tap * 24 + blk
                                return cfm[:, c:c + 1]
                            P.op("dve", "tensor_scalar", out=ac[:], in0=pr[:, 3:TS + 3], scalar1=cw(3), scalar2=None, op0=ALU.mult,
                                 reads=[("pre", 0), ("preh", 0), "cfm"], writes=[("acc", 0)])
                            for tap in (2, 1, 0):
                                P.op("dve", "scalar_tensor_tensor", out=ac[:], in0=pr[:, tap:tap + TS], scalar=cw(tap), in1=ac[:],
                                     op0=ALU.mult, op1=ALU.add,
                                     reads=[("pre", 0), ("preh", 0), ("acc", 0), "cfm"], writes=[("acc", 0)])
                            P.op("act", "activation", out=dst[:, hq, :], in_=ac[:], func=AF.Silu, reads=[("acc", 0)], writes=[(dk, hq)])
                            if c3 < 2:
                                sq_ = GT[c3].sqs[:].rearrange("p a j -> p (a j)")[:, 0:TS]
                                P.op("act", "activation", out=sq_, in_=dst[:, hq, :], func=AF.Square, reads=[(dk, hq)], writes=[("sqs", c3)])
                                bq_ = nsm()
                                for tt in range(ST):
                                    P.op("pe", "matmul", ps[bq_][:, tt:tt + 1], lhsT=sq_[:, tt * 128:(tt + 1) * 128], rhs=onesB[:, 0:1], start=True, stop=True,
                                         reads=[("sqs", c3), "onesB"], writes=[PK(bq_)])
                                P.op("dve", "tensor_scalar", out=rst[:, c3, :, hq], in0=ps[bq_][:, 0:ST], scalar1=EPS, scalar2=None, op0=ALU.add,
                                     writes=[PK(bq_), ("rst", c3, hq)])
                            yield

                def gdn_norms():
                    P.op("pool", "tensor_tensor", out=rr[:].rearrange("p a t h -> p (a t h)"), in0=rst[:].rearrange("p a t h -> p (a t h)"),
                         in1=mhalf.to_broadcast([128, 2 * ST * 8]), op=ALU.pow,
                         reads=[("rst", a_, h_) for a_ in range(2) for h_ in range(8)] + ["konst"], writes=["rr"])

                def gdn_tiles(gr):
                    h0 = gr * NH
                    hsl = slice(h0, h0 + NH)
                    G_ = GT[gr]
                    dT3, sqs, kn_tm, kbd, kdec, vb, knT, Pb, QKT, LcT = G_.dT3, G_.sqs, G_.kn_tm, G_.kbd, G_.kdec, G_.vb, G_.knT, G_.Pb, G_.QKT, G_.LcT
                    LbT, Bc, Ac, Pc, osb, st3 = G_.LbT, G_.Bc, G_.Ac, G_.Pc, G_.osb, G_.st3
                    if True:
                        QK_ = [("qTb", h0 + q) for q in range(NH)]
                        KK_ = [("kTb", h0 + q) for q in range(NH)]
                        VK_ = [("vTb", h0 + q) for q in range(NH)]
                        GZ_ = [("Gz", h0 + q) for q in range(NH)]
                        for tt in range(ST):
                            tsl = slice(tt * 128, (tt + 1) * 128)
                            bt = nsm()
                            P.op("pe", "matmul", ps[bt][0:8, 0:128], lhsT=gg[:, tt, :], rhs=Umat, start=True, stop=True,
                                 reads=["gg", "konst"], writes=[PK(bt)])
                            P.op("pe", "matmul", ps[bt][0:8, 128:256], lhsT=gg[:, tt, :], rhs=Umat, start=True, stop=False,
                                 reads=["gg", "konst"], writes=[PK(bt)])
                            P.op("pe", "matmul", ps[bt][0:8, 128:256], lhsT=glnb[:, tt, :], rhs=identF, start=False, stop=True,
                                 reads=["glnb", "konst"], writes=[PK(bt)])
                            P.op("act", "copy", out=dT3[:, 0:256], in_=ps[bt][0:8, 0:256], writes=[PK(bt), ("dT3a", gr)])
                            sc3 = st3[:, 16:16 + 3 * NH].rearrange("p (h c) -> p h c", c=3)
                            P.op("dve", "tensor_tensor", out=sc3, in0=GP[:, tt, hsl, :], in1=rr[:, 1, tt, hsl].unsqueeze(2).to_broadcast([128, NH, 3]),
                                 op=ALU.mult, reads=["GP", "rr"], writes=[("st3_sc", gr)])
                            b1 = nsm()
                            v1 = ps[b1][:, 0:128 * NH].bitcast(BF16).rearrange("p (h c j) -> p h c j", h=NH, c=2)
                            for hq in range(NH):
                                P.op("pe", "transpose", v1[:, hq, 0, :], kTb[:, h0 + hq, tsl], identB[:], reads=[("kTb", h0 + hq), "identB"], writes=[PK(b1)])
                                P.op("pe", "transpose", v1[:, hq, 1, :], vTb[:, h0 + hq, tsl], identB[:], reads=[("vTb", h0 + hq), "identB"], writes=[PK(b1)])
                            for ci, (dstt, dkey) in enumerate(((kn_tm, ("kn_tm", gr)), (kbd, ("kbd", gr)), (kdec, ("kdec", gr)))):
                                P.op("dve", "tensor_tensor", out=dstt[:], in0=v1[:, :, 0, :], in1=sc3[:, :, ci].unsqueeze(2).to_broadcast([128, NH, 128]),
                                     op=ALU.mult, reads=[("st3_sc", gr)], writes=[PK(b1), dkey])
                            for hq in range(NH):
                                P.op("act", "activation", out=vb[:, hq, :], in_=v1[:, hq, 1, :], func=AF.Copy, scale=gbeta[:, tt, h0 + hq:h0 + hq + 1],
                                     reads=["gbeta"], writes=[PK(b1), ("vb", gr)])
                            yield
                            b2 = nsm()
                            v2 = ps[b2][:, 0:64 * NH].bitcast(BF16).rearrange("p (h j) -> p h j", h=NH)
                            for hq in range(NH):
                                P.op("pe", "transpose", v2[:, hq, :], kn_tm[:, hq, :], identB[:], reads=[("kn_tm", gr), "identB"], writes=[PK(b2)])
                            P.op("act", "copy", out=knT[:], in_=v2, writes=[PK(b2), ("knT", gr)])
                            yield
                            b3 = nsm()
                            for hq in range(NH):
                                P.op("pe", "matmul", ps[b3][:, hq * 128:(hq + 1) * 128], lhsT=knT[:, hq, :], rhs=knT[:, hq, :], start=True, stop=True,
                                     reads=[("knT", gr)], writes=[PK(b3)])
                            b4 = nsm()
                            for hq in range(NH):
                                o4 = ps[b4][:, hq * 128:(hq + 1) * 128]
                                P.op("pe", "matmul", o4, lhsT=esel(h0 + hq), rhs=dT3[0:8, 128:256], start=True, stop=False,
                                     reads=["konst", ("dT3a", gr)], writes=[PK(b4)])
                                P.op("pe", "matmul", o4, lhsT=identB[:], rhs=negsB[:], start=False, stop=True,
                                     reads=["identB", "negsB"], writes=[PK(b4)])
                            LbTf = LbT[:].rearrange("p h j -> p (h j)")
                            for hq in range(NH):
                                P.op("act", "activation", out=LbT[:, hq, :], in_=ps[b4][:, hq * 128:(hq + 1) * 128], func=AF.Exp,
                                     bias=gnegd[:, tt, h0 + hq:h0 + hq + 1], reads=["gnegd"], writes=[PK(b4), ("LbT", gr)])
                            Bcf = Bc[:].rearrange("p h j -> p (h j)")
                            Acf = Ac[:].rearrange("p h j -> p (h j)")
                            Pcf = Pc[:].rearrange("p h j -> p (h j)")
                            P.op("dve", "scalar_tensor_tensor", out=Bcf, in0=ps[b3][:, 0:128 * NH], scalar=-1.0, in1=LbTf, op0=ALU.mult, op1=ALU.mult,
                                 reads=[("LbT", gr)], writes=[PK(b3), ("Bc", gr)])
                            yield
                            b5 = nsm()
                            for hq in range(NH):
                                P.op("pe", "transpose", ps[b5][:, hq * 128:(hq + 1) * 128], Bc[:, hq, :].bitcast(F32), identF, reads=[("Bc", gr), "konst"], writes=[PK(b5)])
                            P.op("act", "copy", out=Acf, in_=ps[b5][:, 0:128 * NH], writes=[PK(b5), ("Ac", gr)])
                            P.op("dve", "tensor_tensor", out=Pc[:], in0=Bc[:].bitcast(F32), in1=identF.unsqueeze(1).to_broadcast([128, NH, 128]), op=ALU.add,
                                 reads=[("Bc", gr), "konst"], writes=[("Pc", gr)])
                            yield
                            for lev in range(6):
                                last = (lev == 5)
                                if not last:
                                    bB = nsm()
                                    for hq in range(NH):
                                        P.op("pe", "matmul", ps[bB][:, hq * 128:(hq + 1) * 128], lhsT=mmv(Ac[:, hq, :]), rhs=mmv(Bc[:, hq, :]), start=True, stop=True,
                                             reads=[("Ac", gr), ("Bc", gr)], writes=[PK(bB)])
                                bA = nsm()
                                for hq in range(NH):
                                    P.op("pe", "matmul", ps[bA][:, hq * 128:(hq + 1) * 128], lhsT=mmv(Bc[:, hq, :]), rhs=mmv(Ac[:, hq, :]), start=True, stop=True,
                                         reads=[("Ac", gr), ("Bc", gr)], writes=[PK(bA)])
                                if not last:
                                    P.op("act", "copy", out=Bcf, in_=ps[bB][:, 0:128 * NH], writes=[PK(bB), ("Bc", gr)])
                                P.op("dve", "tensor_copy", out=Acf, in_=ps[bA][:, 0:128 * NH], writes=[PK(bA), ("Ac", gr)])
                                yield
                                bP = nsm()
                                for hq in range(NH):
                                    P.op("pe", "matmul", ps[bP][:, hq * 128:(hq + 1) * 128], lhsT=mmv(Ac[:, hq, :]), rhs=mmv(Pc[:, hq, :]), start=True, stop=True,
                                         reads=[("Ac", gr), ("Pc", gr)], writes=[PK(bP)])
                                if not last:
                                    P.op("dve", "tensor_tensor", out=Pcf, in0=Pcf.bitcast(F32), in1=ps[bP][:, 0:128 * NH], op=ALU.add,
                                         reads=[("Pc", gr)], writes=[PK(bP), ("Pc", gr)])
                                else:
                                    P.op("dve", "tensor_tensor", out=Pb[:].rearrange("p h j -> p (h j)"), in0=Pcf.bitcast(F32), in1=ps[bP][:, 0:128 * NH], op=ALU.add,
                                         reads=[("Pc", gr)], writes=[PK(bP), ("Pb", gr)])
                                yield
                            b6 = nsm()
                            for hq in range(NH):
                                o6 = ps[b6][:, hq * 128:(hq + 1) * 128]
                                P.op("pe", "matmul", o6, lhsT=esel(h0 + hq), rhs=dT3[0:8, 0:128], start=True, stop=False,
                                     reads=["konst", ("dT3a", gr)], writes=[PK(b6)])
                                P.op("pe", "matmul", o6, lhsT=identB[:], rhs=negcB[:], start=False, stop=True,
                                     reads=["identB", "negcB"], writes=[PK(b6)])
                            for hq in range(NH):
                                P.op("act", "activation", out=LcT[:, hq, :], in_=ps[b6][:, hq * 128:(hq + 1) * 128], func=AF.Exp,
                                     bias=gnegd[:, tt, h0 + hq:h0 + hq + 1], reads=["gnegd"], writes=[PK(b6), ("LcT", gr)])
                            yield
                            b7 = nsm()
                            for hq in range(NH):
                                P.op("pe", "matmul", ps[b7][:, hq * 128:(hq + 1) * 128], lhsT=knT[:, hq, :], rhs=qTb[:, h0 + hq, tsl], start=True, stop=True,
                                     reads=[("knT", gr), ("qTb", h0 + hq)], writes=[PK(b7)])
                            P.op("dve", "tensor_tensor", out=QKT[:].rearrange("p h j -> p (h j)"), in0=ps[b7][:, 0:128 * NH],
                                 in1=LcT[:].rearrange("p h j -> p (h j)"), op=ALU.mult, reads=[("LcT", gr)], writes=[PK(b7), ("QKT", gr)])
                            b8 = nsm()
                            for hq in range(NH):
                                P.op("pe", "matmul", ps[b8][:, hq * 128:(hq + 1) * 128], lhsT=kbd[:, hq, :], rhs=Pb[:, hq, :], start=True, stop=True,
                                     reads=[("kbd", gr), ("Pb", gr)], writes=[PK(b8)])
                            wTn = sqs[:, 0:NH, :]
                            vnew = sqs[:, NH:2 * NH, :]
                            P.op("act", "activation", out=wTn, in_=ps[b8][:, 0:128 * NH].rearrange("p (h j) -> p h j", h=NH), func=AF.Copy, scale=-1.0,
                                 writes=[PK(b8), ("sqs", gr)])
                            yield
                            b9 = nsm()
                            for hq in range(NH):
                                o9 = ps[b9][:, hq * 128:(hq + 1) * 128]
                                P.op("pe", "matmul", o9, lhsT=Pb[:, hq, :], rhs=vb[:, hq, :], start=True, stop=False, reads=[("Pb", gr), ("vb", gr)], writes=[PK(b9)])
                                P.op("pe", "matmul", o9, lhsT=wTn[:, hq, :], rhs=Sb[:, h0 + hq, :], start=False, stop=True,
                                     reads=[("sqs", gr), ("Sb", h0 + hq)], writes=[PK(b9)])
                            P.op("act", "copy", out=vnew, in_=ps[b9][:, 0:128 * NH].rearrange("p (h j) -> p h j", h=NH), reads=[("sqs", gr)], writes=[PK(b9), ("sqs", gr)])
                            yield
                            b10 = nsm()
                            b11 = nsm()
                            for hq in range(NH):
                                P.op("pe", "matmul", ps[b10][:, hq * 128:(hq + 1) * 128], lhsT=qTb[:, h0 + hq, tsl], rhs=Sb[:, h0 + hq, :], start=True, stop=True,
                                     reads=[("qTb", h0 + hq), ("Sb", h0 + hq)], writes=[PK(b10)])
                            for hq in range(NH):
                                P.op("pe", "matmul", ps[b11][:, hq * 128:(hq + 1) * 128], lhsT=QKT[:, hq, :], rhs=vnew[:, hq, :], start=True, stop=True,
                                     reads=[("QKT", gr), ("sqs", gr)], writes=[PK(b11)])
                            P.op("dve", "tensor_tensor", out=LbT[:], in0=ps[b10][:, 0:128 * NH].rearrange("p (h j) -> p h j", h=NH),
                                 in1=gEd[:, tt, hsl].unsqueeze(2).to_broadcast([128, NH, 128]), op=ALU.mult,
                                 reads=["gEd"], writes=[PK(b10), ("LbT", gr)])
                            osbf = osb[:].rearrange("p h j -> p (h j)")
                            P.op("dve", "tensor_tensor", out=osbf, in0=LbTf, in1=ps[b11][:, 0:128 * NH], op=ALU.add,
                                 reads=[("LbT", gr)], writes=[PK(b11), ("LbT", gr)])
                            yield
                            b12 = nsm()
                            for hq in range(NH):
                                P.op("pe", "matmul", ps[b12][:, hq * 128:(hq + 1) * 128], lhsT=kdec[:, hq, :], rhs=vnew[:, hq, :], start=True, stop=True,
                                     reads=[("kdec", gr), ("sqs", gr)], writes=[PK(b12)])
                            SK_ = [("S", l, h0 + q) for q in range(NH)]
                            SBK_ = [("Sb", h0 + q) for q in range(NH)]
                            P.op("dve", "tensor_tensor", out=Sst[:, hsl, :], in0=Sst[:, hsl, :], in1=gEl[:, tt, hsl].unsqueeze(2).to_broadcast([128, NH, 128]),
                                 op=ALU.mult, reads=SK_ + ["gEl"], writes=SK_)
                            P.op("dve", "tensor_tensor", out=Sst[:, hsl, :], in0=Sst[:, hsl, :], in1=ps[b12][:, 0:128 * NH].rearrange("p (h j) -> p h j", h=NH),
                                 op=ALU.add, reads=SK_, writes=[PK(b12)] + SK_)
                            P.op("act", "copy", out=Sb[:, hsl, :], in_=Sst[:, hsl, :], reads=SK_, writes=SBK_)
                            yield
                            for hq in range(NH):
                                P.op("act", "activation", out=kn_tm[:, hq, :], in_=osb[:, hq, :], func=AF.Square, accum_out=st3[:, 28 + hq:29 + hq],
                                     reads=[("LbT", gr)], writes=[("kn_tm", gr), ("st3_o", gr, hq)])
                            SO_ = [("st3_o", gr, q) for q in range(NH)]
                            rq = rr[:, 0, tt, hsl]
                            P.op("dve", "scalar_tensor_tensor", out=st3[:, 32:32 + NH], in0=rq, scalar=1.0 / 16384.0, in1=rq, op0=ALU.mult, op1=ALU.mult,
                                 reads=["rr"], writes=[("st3_r2", gr)])
                            P.op("dve", "tensor_tensor", out=st3[:, 36:36 + NH], in0=st3[:, 28:28 + NH], in1=st3[:, 32:32 + NH], op=ALU.mult,
                                 reads=SO_ + [("st3_r2", gr)], writes=[("st3_m", gr)])
                            P.op("pool", "tensor_tensor", out=st3[:, 36:36 + NH], in0=st3[:, 36:36 + NH], in1=epsc.to_broadcast([128, NH]), op=ALU.add,
                                 reads=[("st3_m", gr), "konst"], writes=[("st3_m", gr)])
                            rsqrt_cols(st3[:, 40:40 + NH], st3[:, 36:36 + NH], NH, [("st3_m", gr)], [("st3_rs", gr)])
                            P.op("dve", "scalar_tensor_tensor", out=st3[:, 44:44 + NH], in0=st3[:, 40:40 + NH], scalar=128.0 ** -0.5, in1=rq,
                                 op0=ALU.mult, op1=ALU.mult, reads=[("st3_rs", gr), "rr"], writes=[("st3_f", gr)])
                            P.op("dve", "tensor_tensor", out=osb[:], in0=osb[:], in1=st3[:, 44:44 + NH].unsqueeze(2).to_broadcast([128, NH, 128]), op=ALU.mult,
                                 reads=[("LbT", gr), ("st3_f", gr)], writes=[("LbT", gr)])
                            P.op("dve", "tensor_tensor", out=kn_tm[:], in0=osb[:], in1=Gz[:, hsl, tt, :], op=ALU.mult,
                                 reads=[("LbT", gr)] + GZ_, writes=[("kn_tm", gr)])
                            yield
                            b13 = nsm()
                            v13 = ps[b13][:, 0:64 * NH].bitcast(BF16).rearrange("p (h j) -> p h j", h=NH)
                            for hq in range(NH):
                                P.op("pe", "transpose", v13[:, hq, :], kn_tm[:, hq, :], identB[:], reads=[("kn_tm", gr), "identB"], writes=[PK(b13)])
                            P.op("act", "copy", out=yT[:, 8 + h0:8 + h0 + NH, tsl], in_=v13,
                                 writes=[PK(b13)] + [("yT", 8 + h0 + q, tt) for q in range(NH)])
                            yield

                def outproj_half(kh):
                    wo = wout_d[l].rearrange("(k p) e -> p k e", p=128)
                    for ch in range(2):
                        sl, Wo = load_M(wo[:, kh * 8:(kh + 1) * 8, ch * 512:(ch + 1) * 512], (8, 512))
                        for tt in range(ST):
                            tsl = slice(tt * 128, (tt + 1) * 128)
                            bb = nbig()
                            for kc in range(8):
                                P.op("pe", "matmul", ps[bb][:, :], lhsT=yT[:, kh * 8 + kc, tsl], rhs=Wo[:, kc, :],
                                     start=(kc == 0), stop=(kc == 7),
                                     reads=[("yT", kh * 8 + c, tt) for c in range(8)] + [("ringM", sl)], writes=[PK(bb)])
                            P.op("dve", "tensor_tensor", out=H[:, tt, ch * 512:(ch + 1) * 512], in0=H[:, tt, ch * 512:(ch + 1) * 512],
                                 in1=ps[bb][:, :], op=ALU.add, reads=[("H", tt)], writes=[PK(bb), ("H", tt)])
                            yield

                def gmlp_stream():
                    wv = win_d[l].rearrange("(k p) e -> p k e", p=128)
                    for ch in range(2):
                        sl, Wm = load_M(wv[:, :, ch * 512:(ch + 1) * 512], (8, 512))
                        for blk in range(4):
                            c = ch * 4 + blk
                            bb = nbig()
                            for kc in range(8):
                                P.op("pe", "matmul", ps[bb][:, :], lhsT=Wm[:, kc, blk * 128:(blk + 1) * 128], rhs=xnT[:, kc, :], start=(kc == 0), stop=(kc == 7),
                                    reads=XT + [("ringM", sl)], writes=[PK(bb)])
                            P.op("act", "activation", out=gu[:, c, :], in_=ps[bb][:, :], func=AF.Gelu_apprx_tanh,
                                 writes=[PK(bb), ("gu", c)])
                            yield
                    for ch in range(2):
                        sl, Wm = load_M(wv[:, :, 2048 + ch * 512:2048 + (ch + 1) * 512], (8, 512))
                        for blk in range(4):
                            c = ch * 4 + blk
                            bb = nbig()
                            for kc in range(8):
                                P.op("pe", "matmul", ps[bb][:, :], lhsT=Wm[:, kc, blk * 128:(blk + 1) * 128], rhs=xnT[:, kc, :], start=(kc == 0), stop=(kc == 7),
                                    reads=XT + [("ringM", sl)], writes=[PK(bb)])
                            P.op("act", "activation", out=xn[0][:, 0:TS], in_=ps[bb][:, :], func=AF.Silu,
                                 writes=[PK(bb), ("xn", 0)])
                            P.op("dve", "tensor_tensor", out=gu[:, c, :], in0=gu[:, c, :], in1=xn[0][:, 0:TS], op=ALU.mult,
                                 reads=[("gu", c), ("xn", 0)], writes=[("gu", c)])
                            yield
                    sl0, Wv0 = load_M(wv[:, :, 1024:1536], (8, 512))
                    sl1, Wv1 = load_M(wv[:, :, 1536:2048], (8, 512))
                    gv = nln
                    for tt in range(ST):
                        tsl = slice(tt * 128, (tt + 1) * 128)
                        bbs = []
                        for half, (slh, Wh) in enumerate(((sl0, Wv0), (sl1, Wv1))):
                            bb = nbig()
                            bbs.append(bb)
                            for kc in range(8):
                                P.op("pe", "matmul", ps[bb][:, :], lhsT=xnT[:, kc, tsl], rhs=Wh[:, kc, :], start=(kc == 0), stop=(kc == 7),
                                    reads=[("xnT", tt), ("ringM", slh)], writes=[PK(bb)])
                            P.op("act", "activation", out=gv[:, half * 512:(half + 1) * 512], in_=ps[bb][:, :],
                                                                                  func=AF.Gelu_apprx_tanh, accum_out=stn[:, 4 + half:5 + half],
                                 writes=[PK(bb), "nln", ("stn4", half)])
                            yield
                        P.op("dve", "tensor_tensor", out=stn[:, 6:7], in0=stn[:, 4:5], in1=stn[:, 5:6], op=ALU.add,
                             reads=[("stn4", 0), ("stn4", 1)], writes=["stn6"])
                        P.op("dve", "tensor_scalar", out=stn[:, 7:8], in0=stn[:, 6:7], scalar1=-1.0 / D_MODEL, scalar2=None, op0=ALU.mult,
                             reads=["stn6"], writes=["stn7"])
                        P.op("act", "activation", out=xn[0][:], in_=gv[:], func=AF.Square, bias=stn[:, 7:8], accum_out=stn[:, 8:9],
                             reads=["nln", "stn7"], writes=[("xn", 0), "stn8"])
                        P.op("dve", "tensor_scalar", out=stn[:, 9:10], in0=stn[:, 8:9], scalar1=1.0 / D_MODEL, scalar2=EPS,
                                                              op0=ALU.mult, op1=ALU.add, reads=["stn8"], writes=["stn9"])
                        rsqrt_cols(stn[:, 10:11], stn[:, 9:10], 1, ["stn9"], ["stn10"])
                        P.op("dve", "tensor_tensor", out=stn[:, 11:12], in0=stn[:, 7:8], in1=stn[:, 10:11], op=ALU.mult,
                             reads=["stn7", "stn10"], writes=["stn11"])
                        P.op("act", "activation", out=nln[:], in_=gv[:], func=AF.Identity, bias=stn[:, 11:12], scale=stn[:, 10:11],
                             reads=["nln", "stn10", "stn11"], writes=["nln"])
                        for half in range(2):
                            bb = nbig()
                            for gq in range(4):
                                g = half * 4 + gq
                                P.op("pe", "matmul", ps[bb][:, gq * 128:(gq + 1) * 128], lhsT=nln[:, g * 128:(g + 1) * 128],
                                                                                 rhs=wsTb[:, g, :], start=True, stop=True,
                                     reads=["nln", "wsTb"], writes=[PK(bb)])
                            psv = ps[bb][:, :].rearrange("p (g i) -> p g i", g=4)
                            P.op("dve", "tensor_tensor", out=tmix[:], in0=psv, in1=cfm[:, C_LNG + half * 4:C_LNG + half * 4 + 4].unsqueeze(2).to_broadcast([128, 4, 128]),
                                op=ALU.mult, reads=["cfm"], writes=[PK(bb), "tmix"])
                            P.op("dve", "tensor_tensor", out=tmix[:], in0=tmix[:], in1=Rt[:, half * 4:half * 4 + 4, :], op=ALU.add,
                                 reads=["tmix", "Rt"], writes=["tmix"])
                            P.op("dve", "tensor_tensor", out=yT[:, half * 4:half * 4 + 4, tsl], in0=tmix[:],
                                                                                       in1=gu[:, half * 4:half * 4 + 4, tsl], op=ALU.mult,
                                 reads=["tmix"] + [("gu", half * 4 + q) for q in range(4)], writes=[("yT", half * 4 + q, tt) for q in range(4)])
                            yield
                    if "outproj" in stages:
                        yield from outproj_half(0)

                def pump(gens, ratio):
                    alive = [True] * len(gens)
                    while any(alive):
                        for gi, (g_, r_) in enumerate(zip(gens, ratio)):
                            if not alive[gi]:
                                continue
                            for _ in range(r_):
                                try:
                                    next(g_)
                                except StopIteration:
                                    alive[gi] = False
                                    break
                ms = gmlp_stream() if "gmlp" in stages else iter(())
                if "gdn" in stages:
                    g1 = gdn_phase1()
                    alive1 = True
                    while alive1:
                        for _ in range(2):
                            try:
                                next(g1)
                            except StopIteration:
                                alive1 = False
                                break
                        try:
                            next(ms)
                        except StopIteration:
                            pass
                    gdn_norms()
                    tg = [gdn_tiles(gr) for gr in range(8 // NH)]
                    ng_ = len(tg)
                    for gi_ in range(ng_ - 1):
                        for _ in range(GOFF * (ng_ - 1 - gi_)):
                            try:
                                next(tg[gi_])
                            except StopIteration:
                                break
                    pump(tg + [ms], [1] * (8 // NH) + [1])
                else:
                    pump([ms], [1])

                if "outproj" in stages:
                    for _ in outproj_half(1):
                        pass

                if "tail" in stages:
                    wpk, Wp = load_Gx(wple_d[l].rearrange("(k p) e -> p k e", p=128), (2, 1024))
                    wg = wgate_d[l].rearrange("(k p) e -> p k e", p=128)
                    slg, Wgs = [], []
                    for half in range(2):
                        sl_, W_ = load_M(wg[:, :, half * 512:(half + 1) * 512], (8, 512))
                        slg.append(sl_)
                        Wgs.append(W_)
                    for tt in range(ST):
                        norm_to_T(l, T0 + tt, tt, C_GNW, xnT)
                    for tt in range(ST):
                        T = T0 + tt
                        tsl = slice(tt * 128, (tt + 1) * 128)
                        pt = acc[0][:, 0:256]
                        P.dma("sp", ("ldp", l % 2), out=pt, in_=p_d[l, (s * ST + T) * 128:(s * ST + T + 1) * 128, :], writes=[("acc", 0)])
                        b = nsm()
                        for kc in range(2):
                            P.op("pe", "transpose", ps[b][:, kc * 128:(kc + 1) * 128], pt[:, kc * 128:(kc + 1) * 128], identF,
                                 reads=[("acc", 0), "konst"], writes=[PK(b)])
                        P.op("act", "copy", out=pT[:, :, tsl], in_=ps[b][:, 0:256].rearrange("p (k j) -> p k j", k=2),
                             writes=[PK(b), ("xn", 0)])
                        ef = TM1
                        for half in range(2):
                            bb = nbig()
                            for kc in range(2):
                                P.op("pe", "matmul", ps[bb][:, :], lhsT=pT[:, kc, tsl], rhs=Wp[:, kc, half * 512:(half + 1) * 512],
                                     start=(kc == 0), stop=(kc == 1), reads=[("xn", 0)] + wpk, writes=[PK(bb)])
                            P.op("act", "copy", out=ef[:, half * 512:(half + 1) * 512], in_=ps[bb][:, :],
                                 writes=[PK(bb), ("TM", 1, half)])
                        P.op("act", "activation", out=nln[:], in_=ef[:], func=AF.Square, accum_out=stn[:, 12:13],
                             reads=[("TM", 1, 0), ("TM", 1, 1)], writes=["nln", "stn12"])
                        P.op("dve", "tensor_scalar", out=stn[:, 13:14], in0=stn[:, 12:13], scalar1=1.0 / D_MODEL, scalar2=EPS,
                             op0=ALU.mult, op1=ALU.add, reads=["stn12"], writes=["stn13"])
                        rsqrt_cols(stn[:, 14:15], stn[:, 13:14], 1, ["stn13"], ["stn14"])
                        P.op("dve", "scalar_tensor_tensor", out=ef[:], in0=ef[:], scalar=stn[:, 14:15], in1=cbc[:, B_PNW:B_PNW + 1024],
                             op0=ALU.mult, op1=ALU.mult,
                             reads=[("TM", 1, 0), ("TM", 1, 1), "stn14", "cbc"], writes=[("TM", 1, 0), ("TM", 1, 1)])
                        for half in range(2):
                            bb = nbig()
                            for kc in range(8):
                                P.op("pe", "matmul", ps[bb][:, :], lhsT=xnT[:, kc, tsl], rhs=Wgs[half][:, kc, :],
                                     start=(kc == 0), stop=(kc == 7), reads=[("xnT", tt), ("ringM", slg[half])], writes=[PK(bb)])
                            P.op("act", "activation", out=gsig[:], in_=ps[bb][:, :], func=AF.Sigmoid, writes=[PK(bb), "tmix"])
                            P.op("dve", "tensor_tensor", out=gsig[:], in0=gsig[:], in1=ef[:, half * 512:(half + 1) * 512], op=ALU.mult,
                                 reads=["tmix", ("TM", 1, half)], writes=["tmix"])
                            P.op("dve", "tensor_tensor", out=H[:, T, half * 512:(half + 1) * 512], in0=H[:, T, half * 512:(half + 1) * 512],
                                 in1=gsig[:], op=ALU.add, reads=[("H", T), "tmix"], writes=[("H", T)])

            P.dma("sp", "c_cbc", out=cbc[:, 0:1024], in_=fnw_d[:, :], writes=["cbc"])
            for T in range(ST):
                ob = TM1
                P.op("act", "activation", out=nln[:], in_=H[:, T, :], func=AF.Square, accum_out=stn[:, 0:1],
                     reads=[("H", T)], writes=["nln", "stn0"])
                P.op("dve", "tensor_scalar", out=stn[:, 1:2], in0=stn[:, 0:1], scalar1=1.0 / D_MODEL, scalar2=EPS,
                     op0=ALU.mult, op1=ALU.add, reads=["stn0"], writes=["stn1"])
                rsqrt_cols(stn[:, 2:3], stn[:, 1:2], 1, ["stn1"], ["stn2"])
                P.op("dve", "scalar_tensor_tensor", out=ob[:], in0=H[:, T, :], scalar=stn[:, 2:3], in1=cbc[:, 0:1024],
                     op0=ALU.mult, op1=ALU.mult, reads=[("H", T), "stn2", "cbc"], writes=[("TM", 1, 0), ("TM", 1, 1)])
                Tg = s * ST + T
                outs.append(P.dma("sp", ("sto", 0), out=out_d[Tg * 128:(Tg + 1) * 128, :], in_=ob[:],
                                  reads=[("TM", 1, 0), ("TM", 1, 1)]))
        P.emit(final_waits=outs)
        nops = len(P.ops)
    return nc, nops


def _konst():
    k = np.zeros((128, NK), np.float32)
    k[:, K_ID:K_ID + 128] = np.eye(128, dtype=np.float32)
    p = np.arange(128)[:, None]
    i = np.arange(128)[None, :]
    k[:, K_U:K_U + 128] = (p <= i)
    k[:, K_NC:K_NC + 128] = np.where(p <= i, 0.0, NEG)
    k[:, K_NS:K_NS + 128] = np.where(p < i, 0.0, NEG)
    k[:, K_ONE:K_ONE + 128] = 1.0
    for h in range(8):
        k[h, K_ES + h * 128:K_ES + (h + 1) * 128] = 1.0
    k[:, K_MH] = -0.5
    k[:, K_EPS] = EPS
    return k


def _layout(inp, layers):
    f = lambda a: np.ascontiguousarray(np.asarray(a, dtype=np.float32))
    Ls = list(layers)
    n = len(Ls)
    cfm = np.zeros((n, 128, NFM), np.float32)
    cbc = np.zeros((n, 128, NBC), np.float32)
    bsp = np.zeros((n, 128, 1024), np.float32)
    wst = np.zeros((n, 128, 8, 128), np.float32)
    for a, l in enumerate(Ls):
        cfm[a, :, C_NW:C_NW + 8] = f(inp["norm_w"][l]).reshape(8, 128).T
        cfm[a, :, C_GNW:C_GNW + 8] = f(inp["ple_gate_norm_w"][l]).reshape(8, 128).T
        cfm[a, :, C_LNG:C_LNG + 8] = f(inp["ln_v_g"][l]).reshape(8, 128).T
        cfm[a, :, C_LNB:C_LNB + 8] = f(inp["ln_v_b"][l]).reshape(8, 128).T
        cw = f(inp["conv_w"][l]).reshape(4, 24, 128)
        cfm[a, :, C_CONV:C_CONV + 96] = cw.transpose(2, 0, 1).reshape(128, 96)
        cbc[a, :, B_PNW:B_PNW + 1024] = f(inp["ple_norm_w"][l])[None, :]
        cbc[a, :, B_GDN:B_GDN + 128] = f(inp["gdn_norm_w"][l])[None, :]
        cbc[a, :, B_DTB:B_DTB + 8] = f(inp["dt_bias"][l])[None, :]
        cbc[a, :, B_ALOG:B_ALOG + 8] = f(inp["A_log"][l])[None, :]
        bsp[a] = f(inp["b_spatial"][l]).reshape(1, 1024)
        wst[a] = f(inp["w_spatial"][l]).transpose(2, 0, 1)
    return cfm, cbc, bsp, wst


_CACHE = {}


def _get(L, emit_h):
    key = (L, emit_h)
    if key not in _CACHE:
        _CACHE[key] = build(L, emit_h=emit_h)[0]
    return _CACHE[key]


FUSED = True


def kernel(x, p, norm_w, w_in, ln_v_g, ln_v_b, w_spatial, b_spatial, conv_w, A_log, dt_bias,
           gdn_norm_w, w_out, w_ple, ple_norm_w, ple_gate_norm_w, w_ple_gate, final_norm_w):
    inp = dict(norm_w=norm_w, ln_v_g=ln_v_g, ln_v_b=ln_v_b, w_spatial=w_spatial, b_spatial=b_spatial, conv_w=conv_w,
               A_log=A_log, dt_bias=dt_bias, gdn_norm_w=gdn_norm_w, ple_norm_w=ple_norm_w, ple_gate_norm_w=ple_gate_norm_w)
    f = lambda a: np.ascontiguousarray(np.asarray(a, dtype=np.float32))
    x = f(x)
    p = f(p)
    w_in, w_out, w_ple, w_gate = f(w_in), f(w_out), f(w_ple), f(w_ple_gate)
    fnw = np.ascontiguousarray(np.broadcast_to(f(final_norm_w)[None, :], (128, D_MODEL)))
    konst = _konst()
    depth = w_in.shape[0]
    nb = x.shape[0]
    if FUSED:
        cfm, cbc, bsp, wst = _layout(inp, range(depth))
        nc = _get(depth, False)
        maps = [dict(x=x[b], p=np.ascontiguousarray(p[:, b]), w_in=w_in, w_out=w_out, w_ple=w_ple, w_gate=w_gate,
                     wsT=wst, bsp=bsp, cfm=cfm, cbc=cbc, fnw=fnw, konst=konst) for b in range(nb)]
        res = run_bass_kernel_spmd(nc, maps, core_ids=list(range(nb)))
        return np.stack([np.asarray(r["out"]) for r in res.results], axis=0).astype(np.float32)
    raise RuntimeError("unfused path removed")
```

```python
import numpy as np
from contextlib import ExitStack
import concourse.bass as bass
import concourse.mybir as mybir
from concourse.bass_utils import run_bass_kernel_spmd

F32 = mybir.dt.float32
BF16 = mybir.dt.bfloat16
AF = mybir.ActivationFunctionType
ALU = mybir.AluOpType

COMPUTE = ("pe", "act", "dve", "pool")

D_MODEL = 1024
SEQ = 2048
NT = 16
ST = 4
NH = 4
NSUP = NT // ST
TS = ST * 128
IN_DIM = 7184
EPS = 1e-6
NEG = -65536.0
GOFF = 11
MS_RATIO = 1
MS_IN_PHASE1 = False
NBIG = 2
CH_DT = mybir.dt.float32r

K_ID = 0
K_U = 128
K_ONE = 256
K_ES = 384
K_MH = 384 + 1024
K_EPS = K_MH + 1
NKS = K_EPS + 1
K_NC = NKS
K_NS = NKS + 128
NK = NKS + 256
C_NW, C_GNW, C_LNG, C_LNB, C_CONV = 0, 8, 16, 24, 32
NFM = 128
B_PNW, B_GDN, B_DTB, B_ALOG = 0, 1024, 1152, 1160
NBC = 1168


class _Op:
    __slots__ = ("eng", "emit", "reads", "writes", "dma_sem", "idx", "pos", "deps",
                 "need_inc", "tick", "dma_cnt", "know")

    def __init__(self, eng, emit, reads, writes, dma_sem):
        self.eng, self.emit, self.reads, self.writes, self.dma_sem = eng, emit, reads, writes, dma_sem
        self.deps = []
        self.need_inc = False
        self.tick = None
        self.dma_cnt = None
        self.know = None


class Prog:
    def __init__(self, nc, epoch=2000):
        self.nc = nc
        self.ops = []
        self.epoch = epoch

    def op(self, eng, fname, *args, reads=(), writes=(), **kw):
        self.ops.append(_Op(eng, (fname, args, kw), tuple(reads), tuple(writes), None))

    def dma(self, queue, sem, reads=(), writes=(), **kw):
        o = _Op(queue, ("dma_start", (), kw), tuple(reads), tuple(writes), sem)
        self.ops.append(o)
        return o

    def analyze(self):
        last_w, readers, pos_ctr, dma_ctr, know = {}, {}, {}, {}, {}
        for i, o in enumerate(self.ops):
            o.idx = i
            if o.dma_sem is not None:
                src = ("dma", o.dma_sem)
                dma_ctr[src] = dma_ctr.get(src, 0) + 1
                o.pos = dma_ctr[src]
            else:
                pos_ctr[o.eng] = pos_ctr.get(o.eng, 0) + 1
                o.pos = pos_ctr[o.eng]

        def src_of(o):
            return ("dma", o.dma_sem) if o.dma_sem is not None else o.eng

        ops = self.ops
        for o in ops:
            cand = set()
            raw = set()
            for k in o.reads:
                w = last_w.get(k)
                if w is not None:
                    cand.add(w)
                    raw.add(w)
            for k in o.writes:
                w = last_w.get(k)
                if w is not None:
                    cand.add(w)
                for r in readers.get(k, ()):
                    cand.add(r)
            ek = know.setdefault(o.eng, {})
            best = {}
            for p in cand:
                po = ops[p]
                if po is o:
                    continue
                if po.dma_sem is None and o.dma_sem is None and po.eng == o.eng:
                    if o.eng == "pe":
                        continue
                s = src_of(po)
                if po.pos <= ek.get(s, 0):
                    continue
                if s not in best or ops[best[s]].pos < po.pos:
                    best[s] = p
            final = []
            for s, p in sorted(best.items(), key=lambda kv: -kv[1]):
                po = ops[p]
                if po.pos <= ek.get(s, 0):
                    continue
                final.append(p)
                po.need_inc = True
                ek[s] = po.pos
                for s2, v2 in po.know.items():
                    if ek.get(s2, 0) < v2:
                        ek[s2] = v2
            o.deps = final
            o.know = dict(ek)
            for k in o.reads:
                readers.setdefault(k, []).append(o.idx)
            for k in o.writes:
                last_w[k] = o.idx
                readers[k] = []
        tick = {}
        for o in ops:
            if o.dma_sem is not None:
                o.dma_cnt = 16 * o.pos
            elif o.need_inc:
                tick[o.eng] = tick.get(o.eng, 0) + 1
                o.tick = tick[o.eng]
        self.nticks = tick
        self.dma_sems = sorted({o.dma_sem for o in ops if o.dma_sem is not None}, key=str)

    def emit(self, final_waits=()):
        nc = self.nc
        self.analyze()
        ops = self.ops
        with ExitStack() as st:
            esems = {}
            for e in COMPUTE:
                n = self.nticks.get(e, 0)
                ne = (n + self.epoch - 1) // self.epoch
                esems[e] = [st.enter_context(nc.semaphore(f"s_{e}_{j}")) for j in range(ne)]
            dsems = {k: st.enter_context(nc.semaphore(f"d_{i}")) for i, k in enumerate(self.dma_sems)}
            block = st.enter_context(nc.Block())

            def sem_for(po):
                if po.dma_sem is not None:
                    return dsems[po.dma_sem], po.dma_cnt
                t = po.tick - 1
                return esems[po.eng][t // self.epoch], (t % self.epoch) + 1

            def run(engname, engobj):
                for o in ops:
                    if o.eng != engname:
                        continue
                    for p in o.deps:
                        s, v = sem_for(ops[p])
                        engobj.wait_ge(s, v)
                    fn, a, kw = o.emit
                    ins = getattr(engobj, fn)(*a, **kw)
                    if o.dma_sem is not None:
                        ins.then_inc(dsems[o.dma_sem], 16)
                    elif o.need_inc:
                        s, _ = sem_for(o)
                        ins.then_inc(s, 1)
                if engname == "sp":
                    for o in final_waits:
                        s, v = sem_for(o)
                        engobj.wait_ge(s, v)

            @block.tensor
            def _(e):
                run("pe", e)

            @block.scalar
            def _(e):
                run("act", e)

            @block.vector
            def _(e):
                run("dve", e)

            @block.gpsimd
            def _(e):
                run("pool", e)

            @block.sync
            def _(e):
                run("sp", e)


ALL_STAGES = ("consts", "A", "prep", "gdn", "gmlp", "outproj", "tail")


def build(L, emit_h=False, chain_dt=F32, stages=ALL_STAGES, nsup=NSUP, gdn_heads=8, gdn_cut=99):
    nc = bass.Bass("TRN2", target_bir_lowering=False)

    def din(name, shape):
        return nc.dram_tensor(name, list(shape), F32, kind="ExternalInput").ap()

    x_d = din("x", [SEQ, D_MODEL])
    p_d = din("p", [L, SEQ, 256])
    win_d = din("w_in", [L, D_MODEL, IN_DIM])
    wout_d = din("w_out", [L, 2048, D_MODEL])
    wple_d = din("w_ple", [L, 256, D_MODEL])
    wgate_d = din("w_gate", [L, D_MODEL, D_MODEL])
    wst_d = din("wsT", [L, 128, 8, 128])
    bsp_d = din("bsp", [L, 128, 1024])
    cfm_d = din("cfm", [L, 128, NFM])
    cbc_d = din("cbc", [L, 128, NBC])
    fnw_d = din("fnw", [128, D_MODEL])
    konst_d = din("konst", [128, NK])
    out_d = nc.dram_tensor("out", [SEQ, D_MODEL], F32, kind="ExternalOutput").ap()
    hout_d = nc.dram_tensor("hout", [SEQ, D_MODEL], F32, kind="ExternalOutput").ap() if emit_h else None

    with ExitStack() as st:
        def sb(name, shape, dt=F32):
            return st.enter_context(nc.sbuf_tensor("sb_" + name, list(shape), dt))

        H = sb("H", [128, ST, D_MODEL])
        konst = sb("konst", [128, NKS])
        identB = sb("identB", [128, 128], BF16)
        negcB = sb("negcB", [128, 128], BF16)
        negsB = sb("negsB", [128, 128], BF16)
        onesB = sb("onesB", [128, 128], BF16)
        cfm = sb("cfm", [128, NFM])
        cbc = sb("cbc", [128, NBC])
        wab = sb("wab", [128, 8, 16], BF16)
        wsTb = sb("wsTb", [128, 8, 128], BF16)
        Rt = sb("Rt", [128, 8, 128], BF16)
        negA = sb("negA", [128, 8])
        xn = [sb(f"xn{i}", [128, D_MODEL], BF16) for i in range(1)]
        xnT = sb("xnT", [128, 8, TS], BF16)
        ringG = [sb(f"ringG{i}", [128, 4096], BF16) for i in range(2)]
        ringM = [sb(f"ringM{i}", [128, 4096], BF16) for i in range(2)]
        gu = sb("gu", [128, 8, TS], BF16)
        TM1 = sb("TM1", [128, D_MODEL])
        nln = sb("nln", [128, D_MODEL], BF16)
        tmix = sb("tmix", [128, 4, 128])
        gsig = tmix[:].rearrange("p g i -> p (g i)")
        yT = sb("yT", [128, 16, TS], BF16)
        pre = [sb(f"pre{i}", [128, TS + 3]) for i in range(1)]
        acc = [sb(f"acc{i}", [128, TS]) for i in range(1)]
        qTb = sb("qTb", [128, 8, TS], BF16)
        kTb = sb("kTb", [128, 8, TS], BF16)
        vTb = sb("vTb", [128, 8, TS], BF16)
        halo = sb("halo", [128, L * 24, 3])
        Gz = sb("Gz", [128, 8, ST, 128], BF16)
        gx1 = sb("gx1", [128, ST, 8])
        gbs = sb("gbs", [128, ST, 8])
        ge1 = sb("ge1", [128, ST, 8])
        gsp = sb("gsp", [128, ST, 8])
        gg = sb("gg", [128, ST, 8])
        glb = sb("glb", [128, ST, 8])
        glnb = sb("glnb", [128, ST, 8])
        gbeta = sb("gbeta", [128, ST, 8])
        gd = sb("gd", [128, ST, 8])
        gnegd = sb("gnegd", [128, ST, 8])
        gEd = sb("gEd", [128, ST, 8])
        gtdl = sb("gtdl", [128, ST, 8])
        gEl = sb("gEl", [128, ST, 8])
        GP = sb("GP", [128, ST, 8, 3])
        class _NS:
            pass
        GT = []
        for gi in range(8 // NH):
            t_ = _NS()
            t_.dT3 = sb(f"dT3_{gi}", [8, 256])
            t_.sqs = sb(f"sqs_{gi}", [128, 2 * NH, 128], BF16)
            for nm in ("kn_tm", "kbd", "kdec", "vb", "knT", "Pb", "QKT", "LcT"):
                setattr(t_, nm, sb(f"{nm}_{gi}", [128, NH, 128], BF16))
            t_.LbT = sb(f"LbT_{gi}", [128, NH, 128])
            t_.Bc = sb(f"Bc_{gi}", [128, NH, 128], CH_DT)
            t_.Ac = sb(f"Ac_{gi}", [128, NH, 128], CH_DT)
            t_.Pc = sb(f"Pc_{gi}", [128, NH, 128], CH_DT)
            t_.osb = t_.LbT
            t_.st3 = sb(f"st3_{gi}", [128, 48])
            GT.append(t_)
        sqs = GT[0].sqs
        stn = sb("stn", [128, 16])
        SstL = [sb(f"Sst{i}", [128, 8, 128]) for i in range(L)]
        Sb = sb("Sb", [128, 8, 128], BF16)
        pT = xn[0][:].rearrange("p (k j) -> p k j", k=2)

        ps = [st.enter_context(nc.psum_tensor(f"ps{i}", [128, 512], F32)) for i in range(8)]

        P = Prog(nc)
        big_ctr = [0]
        small_ctr = [0]

        def nbig():
            b = big_ctr[0] % NBIG
            big_ctr[0] += 1
            return b

        def nsm():
            b = NBIG + small_ctr[0] % (8 - NBIG)
            small_ctr[0] += 1
            return b

        def PK(b):
            return ("ps", b)

        identF = konst[:, K_ID:K_ID + 128]
        Umat = konst[:, K_U:K_U + 128]
        onesF = konst[:, K_ONE:K_ONE + 128]
        mhalf = konst[:, K_MH:K_MH + 1]
        epsc = konst[:, K_EPS:K_EPS + 1]

        def mmv(ap):
            return ap

        def esel(h):
            return konst[0:8, K_ES + h * 128:K_ES + (h + 1) * 128]

        def rsqrt_cols(dst, src, n, rkeys, wkeys):
            P.op("pool", "tensor_tensor", out=dst, in0=src, in1=mhalf.to_broadcast([128, n]), op=ALU.pow,
                 reads=list(rkeys) + ["konst"], writes=wkeys)

        P.dma("sp", "c_konst", out=konst[:], in_=konst_d[:, 0:NKS], writes=["konst"])
        P.op("dve", "tensor_copy", out=identB[:], in_=identF, reads=["konst"], writes=["identB"])
        P.dma("pool", "c_negc", out=negcB[:], in_=konst_d[:, K_NC:K_NC + 128], writes=["negcB"])
        P.dma("pool", "c_negs", out=negsB[:], in_=konst_d[:, K_NS:K_NS + 128], writes=["negsB"])
        P.op("dve", "tensor_copy", out=onesB[:], in_=onesF, reads=["konst"], writes=["onesB"])
        P.op("dve", "memset", GP[:], 1.0, writes=["GP"])
        xv = x_d.rearrange("(t p) d -> p t d", p=128)
        for li in range(L):
            P.op("dve", "memset", SstL[li][:], 0.0, writes=[("S", li, h) for h in range(8)])
        P.op("dve", "memset", halo[:], 0.0, writes=[("halo", b_) for b_ in range(L * 24)])

        gslot = [0]
        mslot = [0]
        cur_l = [0]
        outs = []

        def load_G(l, hd):
            sl = gslot[0] % 2
            gslot[0] += 1
            dst = ringG[sl][:].rearrange("p (k c j) -> p k c j", k=8, c=4)
            src = win_d[l].rearrange("(k p) e -> p k e", p=128)[:, :, 3072:7168].rearrange(
                "p k (c h j) -> p k c h j", c=4, h=8)[:, :, :, hd, :]
            for c in range(4):
                P.dma("pool", ("rG", sl, c, l % 2), out=dst[:, :, c, :], in_=src[:, :, c, :], writes=[("ringG", sl, c)])
            return sl

        def load_Gx(src_ap, shape3):
            sl = gslot[0] % 2
            gslot[0] += 1
            k, c = shape3
            dst = ringG[sl][:, 0:k * c].rearrange("p (k c) -> p k c", k=k)
            keys = [("ringG", sl, q) for q in range(4)]
            P.dma("pool", ("rGx", sl, cur_l[0] % 2), out=dst, in_=src_ap, writes=keys)
            return keys, dst

        def load_M(src_ap, shape3):
            sl = mslot[0] % 2
            mslot[0] += 1
            k, c = shape3
            dst = ringM[sl][:, 0:k * c].rearrange("p (k c) -> p k c", k=k)
            P.dma("pool", ("rM", sl, cur_l[0] % 2), out=dst, in_=src_ap, writes=[("ringM", sl)])
            return sl, dst

        def norm_to_T(l, T, tt, col0, dstT):
            xb = xn[0]
            xk = ("xn", 0)
            P.op("act", "activation", out=xb[:], in_=H[:, T, :], func=AF.Square, accum_out=stn[:, 0:1],
                 reads=[("H", T)], writes=[xk, "stn0"])
            P.op("dve", "tensor_scalar", out=stn[:, 1:2], in0=stn[:, 0:1], scalar1=1.0 / D_MODEL, scalar2=EPS,
                                                  op0=ALU.mult, op1=ALU.add, reads=["stn0"], writes=["stn1"])
            rsqrt_cols(stn[:, 2:3], stn[:, 1:2], 1, ["stn1"], ["stn2"])
            P.op("act", "activation", out=xb[:], in_=H[:, T, :], func=AF.Copy, scale=stn[:, 2:3],
                 reads=[("H", T), "stn2"], writes=[xk])
            b = nsm()
            tpv = ps[b][:, :].bitcast(BF16).rearrange("p (k j) -> p k j", k=8)
            for kc in range(8):
                P.op("pe", "transpose", tpv[:, kc, :], xb[:, kc * 128:(kc + 1) * 128], identB[:],
                     reads=[xk, "identB"], writes=[PK(b)])
            P.op("dve", "tensor_tensor", out=dstT[:, :, tt * 128:(tt + 1) * 128], in0=tpv,
                                                  in1=cfm[:, col0:col0 + 8].unsqueeze(2).to_broadcast([128, 8, 128]),
                                                  op=ALU.mult,
                 reads=["cfm"], writes=[PK(b), ("xnT", tt)])

        for s in range(nsup):
            P.dma("sp", ("ldx", s % 2), out=H[:, :, :], in_=xv[:, s * ST:(s + 1) * ST, :], writes=[("H", t) for t in range(ST)])
            for l in range(L):
                cur_l[0] = l
                Sst = SstL[l]
                if "consts" in stages:
                    P.dma("sp", "c_cfm", out=cfm[:], in_=cfm_d[l], writes=["cfm"])
                    P.dma("sp", "c_cbc", out=cbc[:], in_=cbc_d[l], writes=["cbc"])
                    P.dma("pool", "c_rt", out=Rt[:].rearrange("p g i -> p (g i)"), in_=bsp_d[l], writes=["Rt"])
                    P.dma("pool", "c_wab", out=wab[:], in_=win_d[l].rearrange("(k p) e -> p k e", p=128)[:, :, 7168:7184], writes=["wab"])
                    P.dma("pool", "c_wst", out=wsTb[:], in_=wst_d[l], writes=["wsTb"])
                    P.op("pool", "memset", wsTb[64:128, :, 0:64], 0.0, reads=["wsTb"], writes=["wsTb"])
                    for half in range(2):
                        b = nbig()
                        for gq in range(4):
                            g = half * 4 + gq
                            P.op("pe", "matmul", ps[b][:, gq * 128:(gq + 1) * 128], lhsT=onesB[:],
                                                                             rhs=wsTb[:, g, :], start=True, stop=True,
                                 reads=["onesB", "wsTb"], writes=[PK(b)])
                        for gq in range(4):
                            g = half * 4 + gq
                            P.op("dve", "scalar_tensor_tensor", out=Rt[:, g, :], in0=ps[b][:, gq * 128:(gq + 1) * 128], scalar=cfm[:, C_LNB + g:C_LNB + g + 1],
                                in1=Rt[:, g, :], op0=ALU.mult, op1=ALU.add, reads=["cfm", "Rt"], writes=[PK(b), "Rt"])
                    P.op("act", "activation", out=negA[:], in_=cbc[:, B_ALOG:B_ALOG + 8], func=AF.Exp, reads=["cbc"], writes=["negA"])
                    P.op("dve", "tensor_scalar", out=negA[:], in0=negA[:], scalar1=-1.0, scalar2=None, op0=ALU.mult,
                         reads=["negA"], writes=["negA"])
                    P.op("act", "copy", out=Sb[:], in_=Sst[:], reads=[("S", l, h) for h in range(8)], writes=[("Sb", h) for h in range(8)])

                T0 = 0
                if "A" in stages:
                    for tt in range(ST):
                        norm_to_T(l, T0 + tt, tt, C_NW, xnT)
                XT = [("xnT", tt) for tt in range(ST)]

                if "prep" in stages:
                    b = nsm()
                    for tt in range(ST):
                        for kc in range(8):
                            P.op("pe", "matmul", ps[b][:, tt * 16:(tt + 1) * 16], lhsT=xnT[:, kc, tt * 128:(tt + 1) * 128], rhs=wab[:, kc, :],
                                start=(kc == 0), stop=(kc == 7), reads=[("xnT", tt), "wab"], writes=[PK(b)])
                    abv = ps[b][:, 0:ST * 16].rearrange("p (t c) -> p t c", c=16)
                    P.op("dve", "tensor_tensor", out=gx1[:], in0=abv[:, :, 0:8],
                                                          in1=cbc[:, B_DTB:B_DTB + 8].unsqueeze(1).to_broadcast([128, ST, 8]),
                                                          op=ALU.add, reads=["cbc"], writes=[PK(b), "gx1"])
                    P.op("dve", "tensor_copy", out=gbs[:], in_=abv[:, :, 8:16], writes=[PK(b), "gbs"])
                    P.op("act", "activation", out=ge1[:], in_=gx1[:], func=AF.Exp, reads=["gx1"], writes=["ge1"])
                    P.op("act", "activation", out=gsp[:], in_=ge1[:], func=AF.Ln, bias=1.0, reads=["ge1"], writes=["gsp"])
                    P.op("dve", "tensor_tensor", out=gg[:], in0=gsp[:], in1=negA[:].unsqueeze(1).to_broadcast([128, ST, 8]),
                                                          op=ALU.mult, reads=["gsp", "negA"], writes=["gg"])
                    P.op("act", "activation", out=ge1[:], in_=gbs[:], func=AF.Exp, scale=-1.0, reads=["gbs"], writes=["ge1"])
                    P.op("act", "activation", out=glb[:], in_=ge1[:], func=AF.Ln, bias=1.0, reads=["ge1"], writes=["glb"])
                    P.op("act", "activation", out=gbeta[:], in_=glb[:], func=AF.Exp, scale=-1.0, reads=["glb"], writes=["gbeta"])
                    P.op("dve", "tensor_scalar", out=glnb[:], in0=glb[:], scalar1=-1.0, scalar2=None, op0=ALU.mult,
                         reads=["glb"], writes=["glnb"])
                    bd = nsm()
                    bl = nsm()
                    for tt in range(ST):
                        P.op("pe", "matmul", ps[bd][:, tt * 8:(tt + 1) * 8], lhsT=Umat, rhs=gg[:, tt, :],
                                                                     start=True, stop=True, reads=["konst", "gg"], writes=[PK(bd)])
                    for tt in range(ST):
                        P.op("pe", "matmul", ps[bl][:, tt * 8:(tt + 1) * 8], lhsT=onesF, rhs=gg[:, tt, :],
                                                                     start=True, stop=True, reads=["konst", "gg"], writes=[PK(bl)])
                    gdf = gd[:].rearrange("p t h -> p (t h)")
                    P.op("dve", "tensor_copy", out=gdf, in_=ps[bd][:, 0:ST * 8], writes=[PK(bd), "gd"])
                    P.op("dve", "tensor_scalar", out=gnegd[:], in0=gd[:], scalar1=-1.0, scalar2=None, op0=ALU.mult,
                         reads=["gd"], writes=["gnegd"])
                    P.op("act", "activation", out=gEd[:], in_=gd[:], func=AF.Exp, reads=["gd"], writes=["gEd"])
                    P.op("dve", "tensor_tensor", out=gtdl[:].rearrange("p t h -> p (t h)"), in0=ps[bl][:, 0:ST * 8], in1=gdf,
                                                          op=ALU.subtract, reads=["gd"], writes=[PK(bl), "gtdl"])
                    P.op("act", "activation", out=gEl[:].rearrange("p t h -> p (t h)"), in_=ps[bl][:, 0:ST * 8], func=AF.Exp,
                         writes=[PK(bl), "gEl"])
                    P.op("act", "activation", out=GP[:, :, :, 2], in_=gtdl[:], func=AF.Exp, reads=["gtdl"], writes=["GP"])
                    P.op("dve", "tensor_tensor", out=GP[:, :, :, 1], in0=gbeta[:], in1=gEd[:], op=ALU.mult,
                         reads=["gbeta", "gEd", "GP"], writes=["GP"])

                def gdn_phase1():
                    nxt = load_G(l, 0)
                    for hd in range(8):
                        hq = hd
                        sl = nxt
                        if hd + 1 < 8:
                            nxt = load_G(l, hd + 1)
                        Wc = ringG[sl][:].rearrange("p (k c j) -> p k c j", k=8, c=4)
                        bz = nsm()
                        for tt in range(ST):
                            for kc in range(8):
                                P.op("pe", "matmul", ps[bz][:, tt * 128:(tt + 1) * 128], lhsT=xnT[:, kc, tt * 128:(tt + 1) * 128],
                                     rhs=Wc[:, kc, 3, :], start=(kc == 0), stop=(kc == 7),
                                     reads=[("xnT", tt), ("ringG", sl, 3)], writes=[PK(bz)])
                        P.op("act", "activation", out=Gz[:, hq, :, :].rearrange("p t j -> p (t j)"), in_=ps[bz][:, :], func=AF.Silu,
                             writes=[PK(bz), ("Gz", hq)])
                        P.op("dve", "tensor_tensor", out=Gz[:, hq, :, :], in0=Gz[:, hq, :, :],
                             in1=cbc[:, B_GDN:B_GDN + 128].unsqueeze(1).to_broadcast([128, ST, 128]), op=ALU.mult,
                             reads=[("Gz", hq), "cbc"], writes=[("Gz", hq)])
                        yield
                        for c3, dst, dk in ((0, qTb, "qTb"), (1, kTb, "kTb"), (2, vTb, "vTb")):
                            blk = c3 * 8 + hd
                            hb = l * 24 + blk
                            bb = nbig()
                            for kc in range(8):
                                P.op("pe", "matmul", ps[bb][:, :], lhsT=Wc[:, kc, c3, :], rhs=xnT[:, kc, :], start=(kc == 0), stop=(kc == 7),
                                     reads=XT + [("ringG", sl, c3)], writes=[PK(bb)])
                            pr, ac = pre[0], acc[0]
                            P.op("act", "copy", out=pr[:, 3:TS + 3], in_=ps[bb][:, :], writes=[PK(bb), ("pre", 0)])
                            P.op("dve", "tensor_copy", out=pr[:, 0:3], in_=halo[:, hb, :], reads=[("halo", hb)], writes=[("preh", 0)])
                            P.op("dve", "tensor_copy", out=halo[:, hb, :], in_=pr[:, TS:TS + 3], reads=[("pre", 0)], writes=[("halo", hb)])

                            def cw(tap, blk=blk):
                                c = C_CONV + tap * 24 + blk
                                return cfm[:, c:c + 1]
                            P.op("dve", "tensor_scalar", out=ac[:], in0=pr[:, 3:TS + 3], scalar1=cw(3), scalar2=None, op0=ALU.mult,
                                 reads=[("pre", 0), ("preh", 0), "cfm"], writes=[("acc", 0)])
                            for tap in (2, 1, 0):
                                P.op("dve", "scalar_tensor_tensor", out=ac[:], in0=pr[:, tap:tap + TS], scalar=cw(tap), in1=ac[:],
                                     op0=ALU.mult, op1=ALU.add,
                                     reads=[("pre", 0), ("preh", 0), ("acc", 0), "cfm"], writes=[("acc", 0)])
                            P.op("act", "activation", out=dst[:, hq, :], in_=ac[:], func=AF.Silu, reads=[("acc", 0)], writes=[(dk, hq)])
                            yield

                def gdn_tiles(gr):
                    h0 = gr * NH
                    hsl = slice(h0, h0 + NH)
                    G_ = GT[gr]
                    dT3, sqs, kn_tm, kbd, kdec, vb, knT, Pb, QKT, LcT = G_.dT3, G_.sqs, G_.kn_tm, G_.kbd, G_.kdec, G_.vb, G_.knT, G_.Pb, G_.QKT, G_.LcT
                    LbT, Bc, Ac, Pc, osb, st3 = G_.LbT, G_.Bc, G_.Ac, G_.Pc, G_.osb, G_.st3
                    if True:
                        QK_ = [("qTb", h0 + q) for q in range(NH)]
                        KK_ = [("kTb", h0 + q) for q in range(NH)]
                        VK_ = [("vTb", h0 + q) for q in range(NH)]
                        GZ_ = [("Gz", h0 + q) for q in range(NH)]
                        for tt in range(ST):
                            tsl = slice(tt * 128, (tt + 1) * 128)
                            bt = nsm()
                            P.op("pe", "matmul", ps[bt][0:8, 0:128], lhsT=gg[:, tt, :], rhs=Umat, start=True, stop=True,
                                 reads=["gg", "konst"], writes=[PK(bt)])
                            P.op("pe", "matmul", ps[bt][0:8, 128:256], lhsT=gg[:, tt, :], rhs=Umat, start=True, stop=False,
                                 reads=["gg", "konst"], writes=[PK(bt)])
                            P.op("pe", "matmul", ps[bt][0:8, 128:256], lhsT=glnb[:, tt, :], rhs=identF, start=False, stop=True,
                                 reads=["glnb", "konst"], writes=[PK(bt)])
                            P.op("act", "copy", out=dT3[:, 0:256], in_=ps[bt][0:8, 0:256], writes=[PK(bt), ("dT3a", gr)])
                            P.op("act", "activation", out=sqs[:, 0:NH, :], in_=kTb[:, hsl, tsl], func=AF.Square, reads=KK_, writes=[("sqs", gr)])
                            P.op("act", "activation", out=sqs[:, NH:2 * NH, :], in_=qTb[:, hsl, tsl], func=AF.Square, reads=QK_, writes=[("sqs", gr)])
                            yield
                            bs_ = nsm()
                            for j in range(2 * NH):
                                P.op("pe", "matmul", ps[bs_][:, j:j + 1], lhsT=sqs[:, j, :], rhs=onesB[:, 0:1], start=True, stop=True,
                                     reads=[("sqs", gr), "onesB"], writes=[PK(bs_)])
                            P.op("dve", "tensor_scalar", out=st3[:, 0:2 * NH], in0=ps[bs_][:, 0:2 * NH], scalar1=EPS, scalar2=None, op0=ALU.add,
                                 writes=[PK(bs_), ("st3_a", gr)])
                            rsqrt_cols(st3[:, 8:8 + 2 * NH], st3[:, 0:2 * NH], 2 * NH, [("st3_a", gr)], [("st3_r", gr)])
                            sc3 = st3[:, 16:16 + 3 * NH].rearrange("p (h c) -> p h c", c=3)
                            P.op("dve", "tensor_tensor", out=sc3, in0=GP[:, tt, hsl, :], in1=st3[:, 8:8 + NH].unsqueeze(2).to_broadcast([128, NH, 3]),
                                 op=ALU.mult, reads=["GP", ("st3_r", gr)], writes=[("st3_sc", gr)])
                            b1 = nsm()
                            v1 = ps[b1][:, 0:128 * NH].bitcast(BF16).rearrange("p (h c j) -> p h c j", h=NH, c=2)
                            for hq in range(NH):
                                P.op("pe", "transpose", v1[:, hq, 0, :], kTb[:, h0 + hq, tsl], identB[:], reads=[("kTb", h0 + hq), "identB"], writes=[PK(b1)])
                                P.op("pe", "transpose", v1[:, hq, 1, :], vTb[:, h0 + hq, tsl], identB[:], reads=[("vTb", h0 + hq), "identB"], writes=[PK(b1)])
                            for ci, (dstt, dkey) in enumerate(((kn_tm, ("kn_tm", gr)), (kbd, ("kbd", gr)), (kdec, ("kdec", gr)))):
                                P.op("dve", "tensor_tensor", out=dstt[:], in0=v1[:, :, 0, :], in1=sc3[:, :, ci].unsqueeze(2).to_broadcast([128, NH, 128]),
                                     op=ALU.mult, reads=[("st3_sc", gr)], writes=[PK(b1), dkey])
                            for hq in range(NH):
                                P.op("act", "activation", out=vb[:, hq, :], in_=v1[:, hq, 1, :], func=AF.Copy, scale=gbeta[:, tt, h0 + hq:h0 + hq + 1],
                                     reads=["gbeta"], writes=[PK(b1), ("vb", gr)])
                            yield
                            b2 = nsm()
                            v2 = ps[b2][:, 0:64 * NH].bitcast(BF16).rearrange("p (h j) -> p h j", h=NH)
                            for hq in range(NH):
                                P.op("pe", "transpose", v2[:, hq, :], kn_tm[:, hq, :], identB[:], reads=[("kn_tm", gr), "identB"], writes=[PK(b2)])
                            P.op("act", "copy", out=knT[:], in_=v2, writes=[PK(b2), ("knT", gr)])
                            yield
                            b3 = nsm()
                            for hq in range(NH):
                                P.op("pe", "matmul", ps[b3][:, hq * 128:(hq + 1) * 128], lhsT=knT[:, hq, :], rhs=knT[:, hq, :], start=True, stop=True,
                                     reads=[("knT", gr)], writes=[PK(b3)])
                            b4 = nsm()
                            for hq in range(NH):
                                o4 = ps[b4][:, hq * 128:(hq + 1) * 128]
                                P.op("pe", "matmul", o4, lhsT=esel(h0 + hq), rhs=dT3[0:8, 128:256], start=True, stop=False,
                                     reads=["konst", ("dT3a", gr)], writes=[PK(b4)])
                                P.op("pe", "matmul", o4, lhsT=identB[:], rhs=negsB[:], start=False, stop=True,
                                     reads=["identB", "negsB"], writes=[PK(b4)])
                            LbTf = LbT[:].rearrange("p h j -> p (h j)")
                            for hq in range(NH):
                                P.op("act", "activation", out=LbT[:, hq, :], in_=ps[b4][:, hq * 128:(hq + 1) * 128], func=AF.Exp,
                                     bias=gnegd[:, tt, h0 + hq:h0 + hq + 1], reads=["gnegd"], writes=[PK(b4), ("LbT", gr)])
                            Bcf = Bc[:].rearrange("p h j -> p (h j)")
                            Acf = Ac[:].rearrange("p h j -> p (h j)")
                            Pcf = Pc[:].rearrange("p h j -> p (h j)")
                            P.op("dve", "scalar_tensor_tensor", out=Bcf, in0=ps[b3][:, 0:128 * NH], scalar=-1.0, in1=LbTf, op0=ALU.mult, op1=ALU.mult,
                                 reads=[("LbT", gr)], writes=[PK(b3), ("Bc", gr)])
                            yield
                            b5 = nsm()
                            for hq in range(NH):
                                P.op("pe", "transpose", ps[b5][:, hq * 128:(hq + 1) * 128], Bc[:, hq, :].bitcast(F32), identF, reads=[("Bc", gr), "konst"], writes=[PK(b5)])
                            P.op("act", "copy", out=Acf, in_=ps[b5][:, 0:128 * NH], writes=[PK(b5), ("Ac", gr)])
                            P.op("dve", "tensor_tensor", out=Pc[:], in0=Bc[:].bitcast(F32), in1=identF.unsqueeze(1).to_broadcast([128, NH, 128]), op=ALU.add,
                                 reads=[("Bc", gr), "konst"], writes=[("Pc", gr)])
                            yield
                            for lev in range(6):
                                last = (lev == 5)
                                if not last:
                                    bB = nsm()
                                    for hq in range(NH):
                                        P.op("pe", "matmul", ps[bB][:, hq * 128:(hq + 1) * 128], lhsT=mmv(Ac[:, hq, :]), rhs=mmv(Bc[:, hq, :]), start=True, stop=True,
                                             reads=[("Ac", gr), ("Bc", gr)], writes=[PK(bB)])
                                bA = nsm()
                                for hq in range(NH):
                                    P.op("pe", "matmul", ps[bA][:, hq * 128:(hq + 1) * 128], lhsT=mmv(Bc[:, hq, :]), rhs=mmv(Ac[:, hq, :]), start=True, stop=True,
                                         reads=[("Ac", gr), ("Bc", gr)], writes=[PK(bA)])
                                if not last:
                                    P.op("act", "copy", out=Bcf, in_=ps[bB][:, 0:128 * NH], writes=[PK(bB), ("Bc", gr)])
                                P.op("dve", "tensor_copy", out=Acf, in_=ps[bA][:, 0:128 * NH], writes=[PK(bA), ("Ac", gr)])
                                yield
                                bP = nsm()
                                for hq in range(NH):
                                    P.op("pe", "matmul", ps[bP][:, hq * 128:(hq + 1) * 128], lhsT=mmv(Ac[:, hq, :]), rhs=mmv(Pc[:, hq, :]), start=True, stop=True,
                                         reads=[("Ac", gr), ("Pc", gr)], writes=[PK(bP)])
                                if not last:
                                    P.op("dve", "tensor_tensor", out=Pcf, in0=Pcf.bitcast(F32), in1=ps[bP][:, 0:128 * NH], op=ALU.add,
                                         reads=[("Pc", gr)], writes=[PK(bP), ("Pc", gr)])
                                else:
                                    P.op("dve", "tensor_tensor", out=Pb[:].rearrange("p h j -> p (h j)"), in0=Pcf.bitcast(F32), in1=ps[bP][:, 0:128 * NH], op=ALU.add,
                                         reads=[("Pc", gr)], writes=[PK(bP), ("Pb", gr)])
                                yield
                            b6 = nsm()
                            for hq in range(NH):
                                o6 = ps[b6][:, hq * 128:(hq + 1) * 128]
                                P.op("pe", "matmul", o6, lhsT=esel(h0 + hq), rhs=dT3[0:8, 0:128], start=True, stop=False,
                                     reads=["konst", ("dT3a", gr)], writes=[PK(b6)])
                                P.op("pe", "matmul", o6, lhsT=identB[:], rhs=negcB[:], start=False, stop=True,
                                     reads=["identB", "negcB"], writes=[PK(b6)])
                            for hq in range(NH):
                                P.op("act", "activation", out=LcT[:, hq, :], in_=ps[b6][:, hq * 128:(hq + 1) * 128], func=AF.Exp,
                                     bias=gnegd[:, tt, h0 + hq:h0 + hq + 1], reads=["gnegd"], writes=[PK(b6), ("LcT", gr)])
                            yield
                            b7 = nsm()
                            for hq in range(NH):
                                P.op("pe", "matmul", ps[b7][:, hq * 128:(hq + 1) * 128], lhsT=knT[:, hq, :], rhs=qTb[:, h0 + hq, tsl], start=True, stop=True,
                                     reads=[("knT", gr), ("qTb", h0 + hq)], writes=[PK(b7)])
                            P.op("dve", "tensor_tensor", out=QKT[:].rearrange("p h j -> p (h j)"), in0=ps[b7][:, 0:128 * NH],
                                 in1=LcT[:].rearrange("p h j -> p (h j)"), op=ALU.mult, reads=[("LcT", gr)], writes=[PK(b7), ("QKT", gr)])
                            b8 = nsm()
                            for hq in range(NH):
                                P.op("pe", "matmul", ps[b8][:, hq * 128:(hq + 1) * 128], lhsT=kbd[:, hq, :], rhs=Pb[:, hq, :], start=True, stop=True,
                                     reads=[("kbd", gr), ("Pb", gr)], writes=[PK(b8)])
                            wTn = sqs[:, 0:NH, :]
                            vnew = sqs[:, NH:2 * NH, :]
                            P.op("act", "activation", out=wTn, in_=ps[b8][:, 0:128 * NH].rearrange("p (h j) -> p h j", h=NH), func=AF.Copy, scale=-1.0,
                                 writes=[PK(b8), ("sqs", gr)])
                            yield
                            b9 = nsm()
                            for hq in range(NH):
                                o9 = ps[b9][:, hq * 128:(hq + 1) * 128]
                                P.op("pe", "matmul", o9, lhsT=Pb[:, hq, :], rhs=vb[:, hq, :], start=True, stop=False, reads=[("Pb", gr), ("vb", gr)], writes=[PK(b9)])
                                P.op("pe", "matmul", o9, lhsT=wTn[:, hq, :], rhs=Sb[:, h0 + hq, :], start=False, stop=True,
                                     reads=[("sqs", gr), ("Sb", h0 + hq)], writes=[PK(b9)])
                            P.op("act", "copy", out=vnew, in_=ps[b9][:, 0:128 * NH].rearrange("p (h j) -> p h j", h=NH), reads=[("sqs", gr)], writes=[PK(b9), ("sqs", gr)])
                            yield
                            b10 = nsm()
                            b11 = nsm()
                            for hq in range(NH):
                                P.op("pe", "matmul", ps[b10][:, hq * 128:(hq + 1) * 128], lhsT=qTb[:, h0 + hq, tsl], rhs=Sb[:, h0 + hq, :], start=True, stop=True,
                                     reads=[("qTb", h0 + hq), ("Sb", h0 + hq)], writes=[PK(b10)])
                            for hq in range(NH):
                                P.op("pe", "matmul", ps[b11][:, hq * 128:(hq + 1) * 128], lhsT=QKT[:, hq, :], rhs=vnew[:, hq, :], start=True, stop=True,
                                     reads=[("QKT", gr), ("sqs", gr)], writes=[PK(b11)])
                            P.op("dve", "tensor_tensor", out=LbT[:], in0=ps[b10][:, 0:128 * NH].rearrange("p (h j) -> p h j", h=NH),
                                 in1=gEd[:, tt, hsl].unsqueeze(2).to_broadcast([128, NH, 128]), op=ALU.mult,
                                 reads=["gEd"], writes=[PK(b10), ("LbT", gr)])
                            osbf = osb[:].rearrange("p h j -> p (h j)")
                            P.op("dve", "tensor_tensor", out=osbf, in0=LbTf, in1=ps[b11][:, 0:128 * NH], op=ALU.add,
                                 reads=[("LbT", gr)], writes=[PK(b11), ("LbT", gr)])
                            yield
                            b12 = nsm()
                            for hq in range(NH):
                                P.op("pe", "matmul", ps[b12][:, hq * 128:(hq + 1) * 128], lhsT=kdec[:, hq, :], rhs=vnew[:, hq, :], start=True, stop=True,
                                     reads=[("kdec", gr), ("sqs", gr)], writes=[PK(b12)])
                            SK_ = [("S", l, h0 + q) for q in range(NH)]
                            SBK_ = [("Sb", h0 + q) for q in range(NH)]
                            P.op("dve", "tensor_tensor", out=Sst[:, hsl, :], in0=Sst[:, hsl, :], in1=gEl[:, tt, hsl].unsqueeze(2).to_broadcast([128, NH, 128]),
                                 op=ALU.mult, reads=SK_ + ["gEl"], writes=SK_)
                            P.op("dve", "tensor_tensor", out=Sst[:, hsl, :], in0=Sst[:, hsl, :], in1=ps[b12][:, 0:128 * NH].rearrange("p (h j) -> p h j", h=NH),
                                 op=ALU.add, reads=SK_, writes=[PK(b12)] + SK_)
                            P.op("act", "copy", out=Sb[:, hsl, :], in_=Sst[:, hsl, :], reads=SK_, writes=SBK_)
                            yield
                            for hq in range(NH):
                                P.op("act", "activation", out=kn_tm[:, hq, :], in_=osb[:, hq, :], func=AF.Square, accum_out=st3[:, 28 + hq:29 + hq],
                                     reads=[("LbT", gr)], writes=[("kn_tm", gr), ("st3_o", gr, hq)])
                            SO_ = [("st3_o", gr, q) for q in range(NH)]
                            rq = st3[:, 8 + NH:8 + 2 * NH]
                            P.op("dve", "scalar_tensor_tensor", out=st3[:, 32:32 + NH], in0=rq, scalar=1.0 / 16384.0, in1=rq, op0=ALU.mult, op1=ALU.mult,
                                 reads=[("st3_r", gr)], writes=[("st3_r2", gr)])
                            P.op("dve", "tensor_tensor", out=st3[:, 36:36 + NH], in0=st3[:, 28:28 + NH], in1=st3[:, 32:32 + NH], op=ALU.mult,
                                 reads=SO_ + [("st3_r2", gr)], writes=[("st3_m", gr)])
                            P.op("pool", "tensor_tensor", out=st3[:, 36:36 + NH], in0=st3[:, 36:36 + NH], in1=epsc.to_broadcast([128, NH]), op=ALU.add,
                                 reads=[("st3_m", gr), "konst"], writes=[("st3_m", gr)])
                            rsqrt_cols(st3[:, 40:40 + NH], st3[:, 36:36 + NH], NH, [("st3_m", gr)], [("st3_rs", gr)])
                            P.op("dve", "scalar_tensor_tensor", out=st3[:, 44:44 + NH], in0=st3[:, 40:40 + NH], scalar=128.0 ** -0.5, in1=rq,
                                 op0=ALU.mult, op1=ALU.mult, reads=[("st3_rs", gr), ("st3_r", gr)], writes=[("st3_f", gr)])
                            P.op("dve", "tensor_tensor", out=osb[:], in0=osb[:], in1=st3[:, 44:44 + NH].unsqueeze(2).to_broadcast([128, NH, 128]), op=ALU.mult,
                                 reads=[("LbT", gr), ("st3_f", gr)], writes=[("LbT", gr)])
                            P.op("dve", "tensor_tensor", out=kn_tm[:], in0=osb[:], in1=Gz[:, hsl, tt, :], op=ALU.mult,
                                 reads=[("LbT", gr)] + GZ_, writes=[("kn_tm", gr)])
                            yield
                            b13 = nsm()
                            v13 = ps[b13][:, 0:64 * NH].bitcast(BF16).rearrange("p (h j) -> p h j", h=NH)
                            for hq in range(NH):
                                P.op("pe", "transpose", v13[:, hq, :], kn_tm[:, hq, :], identB[:], reads=[("kn_tm", gr), "identB"], writes=[PK(b13)])
                            P.op("act", "copy", out=yT[:, 8 + h0:8 + h0 + NH, tsl], in_=v13,
                                 writes=[PK(b13)] + [("yT", 8 + h0 + q, tt) for q in range(NH)])
                            yield

                def outproj_half(kh):
                    wo = wout_d[l].rearrange("(k p) e -> p k e", p=128)
                    for ch in range(2):
                        sl, Wo = load_M(wo[:, kh * 8:(kh + 1) * 8, ch * 512:(ch + 1) * 512], (8, 512))
                        for tt in range(ST):
                            tsl = slice(tt * 128, (tt + 1) * 128)
                            bb = nbig()
                            for kc in range(8):
                                P.op("pe", "matmul", ps[bb][:, :], lhsT=yT[:, kh * 8 + kc, tsl], rhs=Wo[:, kc, :],
                                     start=(kc == 0), stop=(kc == 7),
                                     reads=[("yT", kh * 8 + c, tt) for c in range(8)] + [("ringM", sl)], writes=[PK(bb)])
                            P.op("dve", "tensor_tensor", out=H[:, tt, ch * 512:(ch + 1) * 512], in0=H[:, tt, ch * 512:(ch + 1) * 512],
                                 in1=ps[bb][:, :], op=ALU.add, reads=[("H", tt)], writes=[PK(bb), ("H", tt)])
                            yield

                def gmlp_stream():
                    wv = win_d[l].rearrange("(k p) e -> p k e", p=128)
                    for ch in range(2):
                        sl, Wm = load_M(wv[:, :, ch * 512:(ch + 1) * 512], (8, 512))
                        for blk in range(4):
                            c = ch * 4 + blk
                            bb = nbig()
                            for kc in range(8):
                                P.op("pe", "matmul", ps[bb][:, :], lhsT=Wm[:, kc, blk * 128:(blk + 1) * 128], rhs=xnT[:, kc, :], start=(kc == 0), stop=(kc == 7),
                                    reads=XT + [("ringM", sl)], writes=[PK(bb)])
                            P.op("act", "activation", out=gu[:, c, :], in_=ps[bb][:, :], func=AF.Gelu_apprx_tanh,
                                 writes=[PK(bb), ("gu", c)])
                            yield
                    for ch in range(2):
                        sl, Wm = load_M(wv[:, :, 2048 + ch * 512:2048 + (ch + 1) * 512], (8, 512))
                        for blk in range(4):
                            c = ch * 4 + blk
                            bb = nbig()
                            for kc in range(8):
                                P.op("pe", "matmul", ps[bb][:, :], lhsT=Wm[:, kc, blk * 128:(blk + 1) * 128], rhs=xnT[:, kc, :], start=(kc == 0), stop=(kc == 7),
                                    reads=XT + [("ringM", sl)], writes=[PK(bb)])
                            P.op("act", "activation", out=xn[0][:, 0:TS], in_=ps[bb][:, :], func=AF.Silu,
                                 writes=[PK(bb), ("xn", 0)])
                            P.op("dve", "tensor_tensor", out=gu[:, c, :], in0=gu[:, c, :], in1=xn[0][:, 0:TS], op=ALU.mult,
                                 reads=[("gu", c), ("xn", 0)], writes=[("gu", c)])
                            yield
                    sl0, Wv0 = load_M(wv[:, :, 1024:1536], (8, 512))
                    sl1, Wv1 = load_M(wv[:, :, 1536:2048], (8, 512))
                    gv = nln
                    for tt in range(ST):
                        tsl = slice(tt * 128, (tt + 1) * 128)
                        bbs = []
                        for half, (slh, Wh) in enumerate(((sl0, Wv0), (sl1, Wv1))):
                            bb = nbig()
                            bbs.append(bb)
                            for kc in range(8):
                                P.op("pe", "matmul", ps[bb][:, :], lhsT=xnT[:, kc, tsl], rhs=Wh[:, kc, :], start=(kc == 0), stop=(kc == 7),
                                    reads=[("xnT", tt), ("ringM", slh)], writes=[PK(bb)])
                            P.op("act", "activation", out=gv[:, half * 512:(half + 1) * 512], in_=ps[bb][:, :],
                                                                                  func=AF.Gelu_apprx_tanh, accum_out=stn[:, 4 + half:5 + half],
                                 writes=[PK(bb), "nln", ("stn4", half)])
                            yield
                        P.op("dve", "tensor_tensor", out=stn[:, 6:7], in0=stn[:, 4:5], in1=stn[:, 5:6], op=ALU.add,
                             reads=[("stn4", 0), ("stn4", 1)], writes=["stn6"])
                        P.op("dve", "tensor_scalar", out=stn[:, 7:8], in0=stn[:, 6:7], scalar1=-1.0 / D_MODEL, scalar2=None, op0=ALU.mult,
                             reads=["stn6"], writes=["stn7"])
                        P.op("act", "activation", out=xn[0][:], in_=gv[:], func=AF.Square, bias=stn[:, 7:8], accum_out=stn[:, 8:9],
                             reads=["nln", "stn7"], writes=[("xn", 0), "stn8"])
                        P.op("dve", "tensor_scalar", out=stn[:, 9:10], in0=stn[:, 8:9], scalar1=1.0 / D_MODEL, scalar2=EPS,
                                                              op0=ALU.mult, op1=ALU.add, reads=["stn8"], writes=["stn9"])
                        rsqrt_cols(stn[:, 10:11], stn[:, 9:10], 1, ["stn9"], ["stn10"])
                        P.op("dve", "tensor_tensor", out=stn[:, 11:12], in0=stn[:, 7:8], in1=stn[:, 10:11], op=ALU.mult,
                             reads=["stn7", "stn10"], writes=["stn11"])
                        P.op("act", "activation", out=nln[:], in_=gv[:], func=AF.Identity, bias=stn[:, 11:12], scale=stn[:, 10:11],
                             reads=["nln", "stn10", "stn11"], writes=["nln"])
                        for half in range(2):
                            bb = nbig()
                            for gq in range(4):
                                g = half * 4 + gq
                                P.op("pe", "matmul", ps[bb][:, gq * 128:(gq + 1) * 128], lhsT=nln[:, g * 128:(g + 1) * 128],
                                                                                 rhs=wsTb[:, g, :], start=True, stop=True,
                                     reads=["nln", "wsTb"], writes=[PK(bb)])
                            psv = ps[bb][:, :].rearrange("p (g i) -> p g i", g=4)
                            P.op("dve", "tensor_tensor", out=tmix[:], in0=psv, in1=cfm[:, C_LNG + half * 4:C_LNG + half * 4 + 4].unsqueeze(2).to_broadcast([128, 4, 128]),
                                op=ALU.mult, reads=["cfm"], writes=[PK(bb), "tmix"])
                            P.op("dve", "tensor_tensor", out=tmix[:], in0=tmix[:], in1=Rt[:, half * 4:half * 4 + 4, :], op=ALU.add,
                                 reads=["tmix", "Rt"], writes=["tmix"])
                            P.op("dve", "tensor_tensor", out=yT[:, half * 4:half * 4 + 4, tsl], in0=tmix[:],
                                                                                       in1=gu[:, half * 4:half * 4 + 4, tsl], op=ALU.mult,
                                 reads=["tmix"] + [("gu", half * 4 + q) for q in range(4)], writes=[("yT", half * 4 + q, tt) for q in range(4)])
                            yield
                    if "outproj" in stages:
                        yield from outproj_half(0)

                def pump(gens, ratio):
                    alive = [True] * len(gens)
                    while any(alive):
                        for gi, (g_, r_) in enumerate(zip(gens, ratio)):
                            if not alive[gi]:
                                continue
                            for _ in range(r_):
                                try:
                                    next(g_)
                                except StopIteration:
                                    alive[gi] = False
                                    break
                ms = gmlp_stream() if "gmlp" in stages else iter(())
                if "gdn" in stages:
                    g1 = gdn_phase1()
                    alive1 = True
                    while alive1:
                        for _ in range(2):
                            try:
                                next(g1)
                            except StopIteration:
                                alive1 = False
                                break
                        if MS_IN_PHASE1:
                            try:
                                next(ms)
                            except StopIteration:
                                pass
                    tg = [gdn_tiles(gr) for gr in range(8 // NH)]
                    ng_ = len(tg)
                    for gi_ in range(ng_ - 1):
                        for _ in range(GOFF * (ng_ - 1 - gi_)):
                            try:
                                next(tg[gi_])
                            except StopIteration:
                                break
                    pump(tg + [ms], [1] * (8 // NH) + [MS_RATIO])
                else:
                    pump([ms], [1])

                if "outproj" in stages:
                    for _ in outproj_half(1):
                        pass

                if "tail" in stages:
                    wpk, Wp = load_Gx(wple_d[l].rearrange("(k p) e -> p k e", p=128), (2, 1024))
                    wg = wgate_d[l].rearrange("(k p) e -> p k e", p=128)
                    slg, Wgs = [], []
                    for half in range(2):
                        sl_, W_ = load_M(wg[:, :, half * 512:(half + 1) * 512], (8, 512))
                        slg.append(sl_)
                        Wgs.append(W_)
                    for tt in range(ST):
                        norm_to_T(l, T0 + tt, tt, C_GNW, xnT)
                    for tt in range(ST):
                        T = T0 + tt
                        tsl = slice(tt * 128, (tt + 1) * 128)
                        pt = acc[0][:, 0:256]
                        P.dma("sp", ("ldp", l % 2), out=pt, in_=p_d[l, (s * ST + T) * 128:(s * ST + T + 1) * 128, :], writes=[("acc", 0)])
                        b = nsm()
                        for kc in range(2):
                            P.op("pe", "transpose", ps[b][:, kc * 128:(kc + 1) * 128], pt[:, kc * 128:(kc + 1) * 128], identF,
                                 reads=[("acc", 0), "konst"], writes=[PK(b)])
                        P.op("act", "copy", out=pT[:, :, tsl], in_=ps[b][:, 0:256].rearrange("p (k j) -> p k j", k=2),
                             writes=[PK(b), ("xn", 0)])
                        ef = TM1
                        for half in range(2):
                            bb = nbig()
                            for kc in range(2):
                                P.op("pe", "matmul", ps[bb][:, :], lhsT=pT[:, kc, tsl], rhs=Wp[:, kc, half * 512:(half + 1) * 512],
                                     start=(kc == 0), stop=(kc == 1), reads=[("xn", 0)] + wpk, writes=[PK(bb)])
                            P.op("act", "copy", out=ef[:, half * 512:(half + 1) * 512], in_=ps[bb][:, :],
                                 writes=[PK(bb), ("TM", 1, half)])
                        P.op("act", "activation", out=nln[:], in_=ef[:], func=AF.Square, accum_out=stn[:, 12:13],
                             reads=[("TM", 1, 0), ("TM", 1, 1)], writes=["nln", "stn12"])
                        P.op("dve", "tensor_scalar", out=stn[:, 13:14], in0=stn[:, 12:13], scalar1=1.0 / D_MODEL, scalar2=EPS,
                             op0=ALU.mult, op1=ALU.add, reads=["stn12"], writes=["stn13"])
                        rsqrt_cols(stn[:, 14:15], stn[:, 13:14], 1, ["stn13"], ["stn14"])
                        P.op("dve", "scalar_tensor_tensor", out=ef[:], in0=ef[:], scalar=stn[:, 14:15], in1=cbc[:, B_PNW:B_PNW + 1024],
                             op0=ALU.mult, op1=ALU.mult,
                             reads=[("TM", 1, 0), ("TM", 1, 1), "stn14", "cbc"], writes=[("TM", 1, 0), ("TM", 1, 1)])
                        for half in range(2):
                            bb = nbig()
                            for kc in range(8):
                                P.op("pe", "matmul", ps[bb][:, :], lhsT=xnT[:, kc, tsl], rhs=Wgs[half][:, kc, :],
                                     start=(kc == 0), stop=(kc == 7), reads=[("xnT", tt), ("ringM", slg[half])], writes=[PK(bb)])
                            P.op("act", "activation", out=gsig[:], in_=ps[bb][:, :], func=AF.Sigmoid, writes=[PK(bb), "tmix"])
                            P.op("dve", "tensor_tensor", out=gsig[:], in0=gsig[:], in1=ef[:, half * 512:(half + 1) * 512], op=ALU.mult,
                                 reads=["tmix", ("TM", 1, half)], writes=["tmix"])
                            P.op("dve", "tensor_tensor", out=H[:, T, half * 512:(half + 1) * 512], in0=H[:, T, half * 512:(half + 1) * 512],
                                 in1=gsig[:], op=ALU.add, reads=[("H", T), "tmix"], writes=[("H", T)])

            P.dma("sp", "c_cbc", out=cbc[:, 0:1024], in_=fnw_d[:, :], writes=["cbc"])
            for T in range(ST):
                ob = TM1
                P.op("act", "activation", out=nln[:], in_=H[:, T, :], func=AF.Square, accum_out=stn[:, 0:1],
                     reads=[("H", T)], writes=["nln", "stn0"])
                P.op("dve", "tensor_scalar", out=stn[:, 1:2], in0=stn[:, 0:1], scalar1=1.0 / D_MODEL, scalar2=EPS,
                     op0=ALU.mult, op1=ALU.add, reads=["stn0"], writes=["stn1"])
                rsqrt_cols(stn[:, 2:3], stn[:, 1:2], 1, ["stn1"], ["stn2"])
                P.op("dve", "scalar_tensor_tensor", out=ob[:], in0=H[:, T, :], scalar=stn[:, 2:3], in1=cbc[:, 0:1024],
                     op0=ALU.mult, op1=ALU.mult, reads=[("H", T), "stn2", "cbc"], writes=[("TM", 1, 0), ("TM", 1, 1)])
                Tg = s * ST + T
                outs.append(P.dma("sp", ("sto", 0), out=out_d[Tg * 128:(Tg + 1) * 128, :], in_=ob[:],
                                  reads=[("TM", 1, 0), ("TM", 1, 1)]))
        P.emit(final_waits=outs)
        nops = len(P.ops)
    return nc, nops


def _konst():
    k = np.zeros((128, NK), np.float32)
    k[:, K_ID:K_ID + 128] = np.eye(128, dtype=np.float32)
    p = np.arange(128)[:, None]
    i = np.arange(128)[None, :]
    k[:, K_U:K_U + 128] = (p <= i)
    k[:, K_NC:K_NC + 128] = np.where(p <= i, 0.0, NEG)
    k[:, K_NS:K_NS + 128] = np.where(p < i, 0.0, NEG)
    k[:, K_ONE:K_ONE + 128] = 1.0
    for h in range(8):
        k[h, K_ES + h * 128:K_ES + (h + 1) * 128] = 1.0
    k[:, K_MH] = -0.5
    k[:, K_EPS] = EPS
    return k


def _layout(inp, layers):
    f = lambda a: np.ascontiguousarray(np.asarray(a, dtype=np.float32))
    Ls = list(layers)
    n = len(Ls)
    cfm = np.zeros((n, 128, NFM), np.float32)
    cbc = np.zeros((n, 128, NBC), np.float32)
    bsp = np.zeros((n, 128, 1024), np.float32)
    wst = np.zeros((n, 128, 8, 128), np.float32)
    for a, l in enumerate(Ls):
        cfm[a, :, C_NW:C_NW + 8] = f(inp["norm_w"][l]).reshape(8, 128).T
        cfm[a, :, C_GNW:C_GNW + 8] = f(inp["ple_gate_norm_w"][l]).reshape(8, 128).T
        cfm[a, :, C_LNG:C_LNG + 8] = f(inp["ln_v_g"][l]).reshape(8, 128).T
        cfm[a, :, C_LNB:C_LNB + 8] = f(inp["ln_v_b"][l]).reshape(8, 128).T
        cw = f(inp["conv_w"][l]).reshape(4, 24, 128)
        cfm[a, :, C_CONV:C_CONV + 96] = cw.transpose(2, 0, 1).reshape(128, 96)
        cbc[a, :, B_PNW:B_PNW + 1024] = f(inp["ple_norm_w"][l])[None, :]
        cbc[a, :, B_GDN:B_GDN + 128] = f(inp["gdn_norm_w"][l])[None, :]
        cbc[a, :, B_DTB:B_DTB + 8] = f(inp["dt_bias"][l])[None, :]
        cbc[a, :, B_ALOG:B_ALOG + 8] = f(inp["A_log"][l])[None, :]
        bsp[a] = f(inp["b_spatial"][l]).reshape(1, 1024)
        wst[a] = f(inp["w_spatial"][l]).transpose(2, 0, 1)
    return cfm, cbc, bsp, wst


_CACHE = {}


def _get(L, emit_h):
    key = (L, emit_h)
    if key not in _CACHE:
        _CACHE[key] = build(L, emit_h=emit_h)[0]
    return _CACHE[key]


FUSED = True


def kernel(x, p, norm_w, w_in, ln_v_g, ln_v_b, w_spatial, b_spatial, conv_w, A_log, dt_bias,
           gdn_norm_w, w_out, w_ple, ple_norm_w, ple_gate_norm_w, w_ple_gate, final_norm_w):
    inp = dict(norm_w=norm_w, ln_v_g=ln_v_g, ln_v_b=ln_v_b, w_spatial=w_spatial, b_spatial=b_spatial, conv_w=conv_w,
               A_log=A_log, dt_bias=dt_bias, gdn_norm_w=gdn_norm_w, ple_norm_w=ple_norm_w, ple_gate_norm_w=ple_gate_norm_w)
    f = lambda a: np.ascontiguousarray(np.asarray(a, dtype=np.float32))
    x = f(x)
    p = f(p)
    w_in, w_out, w_ple, w_gate = f(w_in), f(w_out), f(w_ple), f(w_ple_gate)
    fnw = np.ascontiguousarray(np.broadcast_to(f(final_norm_w)[None, :], (128, D_MODEL)))
    konst = _konst()
    depth = w_in.shape[0]
    nb = x.shape[0]
    if FUSED:
        cfm, cbc, bsp, wst = _layout(inp, range(depth))
        nc = _get(depth, False)
        maps = [dict(x=x[b], p=np.ascontiguousarray(p[:, b]), w_in=w_in, w_out=w_out, w_ple=w_ple, w_gate=w_gate,
                     wsT=wst, bsp=bsp, cfm=cfm, cbc=cbc, fnw=fnw, konst=konst) for b in range(nb)]
        res = run_bass_kernel_spmd(nc, maps, core_ids=list(range(nb)))
        return np.stack([np.asarray(r["out"]) for r in res.results], axis=0).astype(np.float32)
    raise RuntimeError("unfused path removed")
```
